# Optimizing a Trainium2 kernel written in Bass

```python
import math
import jax, jax.numpy as jnp
from jax import lax
import numpy as np

D_MODEL = 1024
BATCH = 16
SEQ = 2048
DEPTH = 1

HEAD_DIM = 64
N_HEADS_MOBA = 8
N_HEADS_SB = 8
WIDTH_MOBA = N_HEADS_MOBA * HEAD_DIM
WIDTH_SB = N_HEADS_SB * HEAD_DIM
MOBA_BLOCK = 256
MOBA_TOPK = 3
MOBA_Q_CHUNK = 16
SB_Q_BLOCK = 128
REL_BUCKETS = 32
REL_MAX_DIST = 128
NORM_EPS = 1e-6
NEG_INF = -1e30
PROJ_SPLITS = [WIDTH_MOBA] * 4 + [WIDTH_SB] * 4 + [D_MODEL] * 2
IN_WIDTH = sum(PROJ_SPLITS)
SPLIT_POINTS = [int(c) for c in np.cumsum(PROJ_SPLITS)[:-1]]

kernel_name = "hybrid_moba_stickbreaking_block"


def rms_norm(x, w):
    xf = x.astype(jnp.float32)
    y = xf * lax.rsqrt(jnp.mean(xf * xf, axis=-1, keepdims=True) + NORM_EPS)
    return (y * w.astype(jnp.float32)).astype(x.dtype)


def split_heads(t, n_heads):
    b, s, _ = t.shape
    return t.reshape(b, s, n_heads, HEAD_DIM).transpose(0, 2, 1, 3)


def merge_heads(t):
    b, h, s, dh = t.shape
    return t.transpose(0, 2, 1, 3).reshape(b, s, h * dh)


def t5_bucket(dist):
    n = jnp.maximum(dist, 0)
    max_exact = REL_BUCKETS // 2
    nf = jnp.maximum(n, 1).astype(jnp.float32)
    large = max_exact + (jnp.log(nf / max_exact) / math.log(REL_MAX_DIST / max_exact)
                         * (REL_BUCKETS - max_exact)).astype(jnp.int32)
    large = jnp.minimum(large, REL_BUCKETS - 1)
    return jnp.where(n < max_exact, n, large)


def moba_attention(q, k, v, rel_bias):
    b, h, s, dh = q.shape
    nb = -(-s // MOBA_BLOCK)
    s_pad = nb * MOBA_BLOCK
    top = min(MOBA_TOPK, nb)
    scale = dh ** -0.5
    pad = ((0, 0), (0, 0), (0, s_pad - s), (0, 0))
    q, k, v = jnp.pad(q, pad), jnp.pad(k, pad), jnp.pad(v, pad)
    k_blk = k.reshape(b, h, nb, MOBA_BLOCK, dh)
    v_blk = v.reshape(b, h, nb, MOBA_BLOCK, dh)
    k_mean = jnp.mean(k_blk.astype(jnp.float32), axis=3)
    n_chunks = s_pad // MOBA_Q_CHUNK
    q_chunks = q.reshape(b, h, n_chunks, MOBA_Q_CHUNK, dh).transpose(2, 0, 1, 3, 4)
    bi = jnp.arange(b)[:, None, None, None]
    hi = jnp.arange(h)[None, :, None, None]
    offs = jnp.arange(MOBA_BLOCK)
    blk_ids = jnp.arange(nb)

    def one_chunk(args):
        qc, ci = args
        t = ci * MOBA_Q_CHUNK + jnp.arange(MOBA_Q_CHUNK)
        cur = (ci * MOBA_Q_CHUNK) // MOBA_BLOCK
        gate = jnp.einsum('bhqd,bhnd->bhqn', qc.astype(jnp.float32), k_mean)
        gate = jnp.where(blk_ids < cur, gate, NEG_INF)
        _, idx = lax.top_k(gate, top)
        valid = idx < cur
        k_sel = k_blk[bi, hi, idx]
        v_sel = v_blk[bi, hi, idx]
        pos_sel = idx[..., None] * MOBA_BLOCK + offs
        bias_sel = rel_bias[hi[..., None], t5_bucket(t[:, None, None] - pos_sel)]
        logit_sel = jnp.einsum('bhqd,bhqjkd->bhqjk', qc, k_sel).astype(jnp.float32) * scale + bias_sel
        logit_sel = jnp.where(valid[..., None], logit_sel, NEG_INF).reshape(b, h, MOBA_Q_CHUNK, top * MOBA_BLOCK)
        k_own = lax.dynamic_index_in_dim(k_blk, cur, axis=2, keepdims=False)
        v_own = lax.dynamic_index_in_dim(v_blk, cur, axis=2, keepdims=False)
        dist_own = t[:, None] - (cur * MOBA_BLOCK + offs)[None, :]
        bias_own = rel_bias[:, t5_bucket(dist_own)]
        logit_own = jnp.einsum('bhqd,bhkd->bhqk', qc, k_own).astype(jnp.float32) * scale + bias_own
        logit_own = jnp.where(dist_own >= 0, logit_own, NEG_INF)
        p = jax.nn.softmax(jnp.concatenate([logit_sel, logit_own], axis=-1), axis=-1).astype(v.dtype)
        p_sel = p[..., :top * MOBA_BLOCK].reshape(b, h, MOBA_Q_CHUNK, top, MOBA_BLOCK)
        p_own = p[..., top * MOBA_BLOCK:]
        return (jnp.einsum('bhqjk,bhqjkd->bhqd', p_sel, v_sel)
                + jnp.einsum('bhqk,bhkd->bhqd', p_own, v_own))

    out = lax.map(one_chunk, (q_chunks, jnp.arange(n_chunks)))
    out = out.transpose(1, 2, 0, 3, 4).reshape(b, h, s_pad, dh)
    return out[:, :, :s]


def stick_breaking_attention(q, k, v):
    s_len = q.shape[2]
    scale = q.shape[-1] ** -0.5
    outs = []
    for i in range(s_len // SB_Q_BLOCK):
        t0, t1 = i * SB_Q_BLOCK, (i + 1) * SB_Q_BLOCK
        z = jnp.einsum('bhqd,bhkd->bhqk', q[:, :, t0:t1], k[:, :, :t1]).astype(jnp.float32) * scale
        past = jnp.arange(t1)[None, :] < (t0 + jnp.arange(SB_Q_BLOCK))[:, None]
        log_1m = jnp.where(past, jax.nn.log_sigmoid(-z), 0.0)
        shifted = jnp.concatenate([log_1m[..., 1:], jnp.zeros_like(log_1m[..., :1])], axis=-1)
        suffix = lax.cumsum(shifted, axis=3, reverse=True)
        w = jnp.where(past, jnp.exp(jax.nn.log_sigmoid(z) + suffix), 0.0).astype(v.dtype)
        outs.append(jnp.einsum('bhqk,bhkd->bhqd', w, v[:, :, :t1]))
    return jnp.concatenate(outs, axis=2)


def setup_inputs(seed: int = 0) -> dict:
    key = jax.random.key(seed)
    ks = jax.random.split(key, 10)
    f32 = jnp.float32
    x = jax.random.normal(ks[0], (BATCH, SEQ, D_MODEL), f32)
    norm_w = 1.0 + 0.02 * jax.random.normal(ks[1], (DEPTH, D_MODEL), f32)
    w_in = jax.random.normal(ks[2], (DEPTH, D_MODEL, IN_WIDTH), f32) * D_MODEL ** -0.5
    merge_gate_b = 0.01 * jax.random.normal(ks[3], (DEPTH, 2 * D_MODEL), f32)
    q_norm_w = 1.0 + 0.02 * jax.random.normal(ks[4], (DEPTH, HEAD_DIM), f32)
    k_norm_w = 1.0 + 0.02 * jax.random.normal(ks[5], (DEPTH, HEAD_DIM), f32)
    rel_bias = 0.2 * jax.random.normal(ks[6], (N_HEADS_MOBA, REL_BUCKETS), f32)
    w_up_moba = jax.random.normal(ks[7], (DEPTH, WIDTH_MOBA, D_MODEL), f32) * WIDTH_MOBA ** -0.5
    w_up_sb = jax.random.normal(ks[8], (DEPTH, WIDTH_SB, D_MODEL), f32) * WIDTH_SB ** -0.5
    w_out = jax.random.normal(ks[9], (DEPTH, D_MODEL, D_MODEL), f32) * D_MODEL ** -0.5
    return {"x": x, "norm_w": norm_w, "w_in": w_in, "merge_gate_b": merge_gate_b,
            "q_norm_w": q_norm_w, "k_norm_w": k_norm_w, "rel_bias": rel_bias,
            "w_up_moba": w_up_moba, "w_up_sb": w_up_sb, "w_out": w_out}


def reference(x, norm_w, w_in, merge_gate_b, q_norm_w, k_norm_w, rel_bias, w_up_moba, w_up_sb, w_out):
    for l in range(DEPTH):
        h = rms_norm(x, norm_w[l])
        proj = h @ w_in[l]
        qa, ka, va, za, qb, kb, vb, zb, gl_a, gl_b = jnp.split(proj, SPLIT_POINTS, axis=-1)
        qa = rms_norm(split_heads(qa, N_HEADS_MOBA), q_norm_w[l])
        ka = rms_norm(split_heads(ka, N_HEADS_MOBA), k_norm_w[l])
        oa = moba_attention(qa, ka, split_heads(va, N_HEADS_MOBA), rel_bias)
        ob = stick_breaking_attention(split_heads(qb, N_HEADS_SB), split_heads(kb, N_HEADS_SB),
                                      split_heads(vb, N_HEADS_SB))
        ya = (merge_heads(oa) * jax.nn.silu(za)) @ w_up_moba[l]
        yb = (merge_heads(ob) * jax.nn.silu(zb)) @ w_up_sb[l]
        gb = merge_gate_b[l]
        y = jax.nn.sigmoid(gl_a + gb[:D_MODEL]) * ya + jax.nn.sigmoid(gl_b + gb[D_MODEL:]) * yb
        x = x + y @ w_out[l]
    return x
```

```python
import numpy as np
import ml_dtypes
import concourse.bass as bass
import concourse.mybir as mybir
from concourse.bass_utils import run_bass_kernel_spmd

F32 = mybir.dt.float32
BF16 = mybir.dt.bfloat16
AF = mybir.ActivationFunctionType
ALU = mybir.AluOpType
AX = mybir.AxisListType

S = 2048
D = 1024
NEG = -30000.0
COMPUTE = ("pe", "act", "dve", "pool")


class Op:
    __slots__ = ("eng", "fn", "seq", "waits", "signal", "dma", "sig_idx")


class Prog:
    def __init__(self):
        self.ops = {e: [] for e in COMPUTE + ("sp",)}
        self.last_w = {}
        self.readers = {}
        self.alias = {}
        self.known = {e: {} for e in COMPUTE + ("sp",)}
        self.dma_count = {}

    def _dep(self, op, prod):
        if prod is op:
            return
        if prod.dma is not None:
            key, val = ("dma", prod.dma[0]), self.dma_count[prod.dma[0]]
        else:
            if prod.eng == op.eng and op.dma is None and op.eng == "pe":
                return
            key, val = prod.eng, prod.seq
            prod.signal = True
        if self.known[op.eng].get(key, 0) >= val:
            return
        if op.waits.get(key, 0) < val:
            op.waits[key] = val

    def _add(self, eng, fn, r, w, x, dma):
        op = Op()
        op.eng, op.fn, op.waits, op.signal, op.dma, op.sig_idx = eng, fn, {}, False, dma, 0
        lst = self.ops[eng]
        op.seq = len(lst) + 1
        wkeys = []
        for k in tuple(w) + tuple(x):
            wkeys.append(k)
            for a in self.alias.get(k, ()):
                wkeys.append(a)
        for k in r:
            p = self.last_w.get(k)
            if p is not None:
                if p.eng == eng and p.dma is None and dma is None and eng != "pe":
                    p.signal = True
                    if self.known[eng].get(eng, 0) < p.seq and op.waits.get(eng, 0) < p.seq:
                        op.waits[eng] = p.seq
                else:
                    self._dep(op, p)
        for k in wkeys:
            p = self.last_w.get(k)
            if p is not None:
                self._dep(op, p)
            for q in self.readers.get(k, ()):
                self._dep(op, q)
        for k in x:
            pass
        for k in r:
            self.readers.setdefault(k, []).append(op)
        for k in wkeys:
            self.last_w[k] = op
            self.readers[k] = []
        for key, val in op.waits.items():
            if self.known[eng].get(key, 0) < val:
                self.known[eng][key] = val
        lst.append(op)
        return op

    def op(self, eng, fn, r=(), w=(), x=()):
        return self._add(eng, fn, r, w, x, None)

    def dma(self, eng, semname, fn, r=(), w=()):
        c = self.dma_count.get(semname, 0) + 16
        self.dma_count.setdefault(semname, 0)
        op = self._add(eng, fn, r, w, (), (semname, c))
        self.dma_count[semname] = c
        return op


def _bucket_table(maxd):
    n = np.arange(maxd, dtype=np.int64)
    nf = np.maximum(n, 1).astype(np.float32)
    large = 16 + (np.log(nf / np.float32(16)) / np.float32(np.log(128 / 16)) * np.float32(16)).astype(np.int32)
    large = np.minimum(large, 31)
    return np.where(n < 16, n, large).astype(np.int64)


class _Stop(Exception):
    pass


def build(nseq, debug=False, stop=99):
    nc = bass.Bass("TRN2", target_bir_lowering=False)
    P = Prog()

    def din(name, shape, dt=F32):
        return nc.dram_tensor(name, list(shape), dt, kind="ExternalInput").ap()

    x_d = din("x", [nseq * S, D])
    win_d = din("w_in", [D, 6144])
    wua_d = din("w_up_a", [512, D])
    wub_d = din("w_up_b", [512, D])
    wo_d = din("w_out", [D, D])
    normw_d = din("normw", [128, 8])
    qkw_d = din("qkw", [128, 2])
    mgb_d = din("mgb", [128, 16])
    biasD_d = din("biasD", [128, 8, 128])
    biasO_d = din("biasO", [128, 8, 128])
    b31_d = din("b31", [128, 8])
    b31g_d = din("b31g", [128, 8])
    cst_d = din("cst", [128, 4])
    ident_d = din("ident", [128, 128], BF16)
    bones_d = din("bones", [128, 128], BF16)
    tri_d = din("tri", [128, 128], BF16)
    ones_d = din("ones", [128, 128], BF16)
    smask_d = din("smask", [128, 128])
    esel_d = din("esel", [128, 8, 128], BF16)
    cmneg_d = din("cmneg", [128, 8, 8])
    cm01_d = din("cm01", [128, 8, 8])
    csel_d = din("csel", [128, 2, 128], BF16)
    rsel_d = din("rsel", [128, 2, 128], BF16)
    negm_d = din("negm", [128, 128], BF16)
    out_d = nc.dram_tensor("out", [nseq * S, D], F32, kind="ExternalOutput").ap()
    dbg = {}
    if debug:
        for nm in ("qa", "ka", "qb", "kb", "oa", "ob"):
            dbg[nm] = nc.dram_tensor("dbg_" + nm, [128, 4, S], BF16, kind="ExternalOutput").ap()
        for nm in ("va", "vb"):
            dbg[nm] = nc.dram_tensor("dbg_" + nm, [128, 16, 512], BF16, kind="ExternalOutput").ap()
        dbg["mask"] = nc.dram_tensor("dbg_mask", [128, 4, S], BF16, kind="ExternalOutput").ap()

    from contextlib import ExitStack
    es = ExitStack()
    with es:
        def sb(name, shape, dt=F32):
            return es.enter_context(nc.sbuf_tensor(name, list(shape), dt))

        xT = sb("xT", [128, 8, S], BF16)
        qa = sb("qa", [128, 4, S], BF16)
        qb_ = sb("qb", [128, 4, S], BF16)
        R = sb("R", [128, 16384], BF16)
        kk = R[:, 0:8192].rearrange("p (c t) -> p c t", c=4)
        VV = R[:, 8192:16384].rearrange("p (t f) -> p t f", t=16)
        wupa = R[:, 0:4096].rearrange("p (c n) -> p c n", c=4)
        wupb = R[:, 4096:8192].rearrange("p (c n) -> p c n", c=4)
        wout = R[:, 8192:16384].rearrange("p (c n) -> p c n", c=8)
        NWS = 2
        wsl = [sb(f"wsl{i}", [128, 8, 512], BF16)[:] for i in range(NWS)]
        maskT = sb("maskT", [128, 4, S], BF16)
        _mflat = maskT[:].rearrange("p c t -> p (c t)")
        wsl = wsl + [_mflat[:, i * 4096:(i + 1) * 4096].rearrange("p (c n) -> p c n", c=8) for i in range(2)]
        normw = sb("normw_s", [128, 8])
        qkw = sb("qkw_s", [128, 2])
        mgb = sb("mgb_s", [128, 16])
        biasD = sb("biasD_s", [128, 8, 128])
        biasO = sb("biasO_s", [128, 8, 128])
        b31 = sb("b31_s", [128, 8])
        cst = sb("cst_s", [128, 4])
        ident = sb("ident_s", [128, 128], BF16)
        bones = sb("bones_s", [128, 128], BF16)
        tri = sb("tri_s", [128, 128], BF16)
        ones = sb("ones_s", [128, 128], BF16)
        smask = sb("smask_s", [128, 128])
        esel = sb("esel_s", [128, 8, 128], BF16)
        cmneg = sb("cmneg_s", [128, 8, 8])
        cm01 = sb("cm01_s", [128, 8, 8])
        csel = sb("csel_s", [128, 2, 128], BF16)
        rsel = sb("rsel_s", [128, 2, 128], BF16)
        negm = sb("negm_s", [128, 128], BF16)
        xt = [sb(f"xt{i}", [128, D]) for i in range(2)]
        hbf = [sb("hbf0", [128, D], BF16)] * 2
        stat = [sb(f"stat{i}", [128, 4]) for i in range(2)]
        NFT, NBT = 10, 8
        FTall = sb("ftall", [128, NFT * 512])
        BTall = sb("btall", [128, NBT * 512], BF16)
        FT = [FTall[:, i * 512:(i + 1) * 512] for i in range(NFT)]
        BT = [BTall[:, i * 512:(i + 1) * 512] for i in range(NBT)]
        sqj = BTall[:, 0:1024]
        Gt = [BTall[:, br * 2048:(br + 1) * 2048].rearrange("p (c t) -> p c t", c=4) for br in range(2)]
        yT = FTall[:, 0:2048].bitcast(BF16).rearrange("p (f t) -> p f t", f=8)
        FTP = lambda i: FTall[:, i * 512:(i + 2) * 512].rearrange("p (a f) -> p a f", a=2)
        BTP = lambda i: BTall[:, i * 512:(i + 2) * 512].rearrange("p (a f) -> p a f", a=2)
        kmean = sb("kmean", [128, 4, 8])
        gm = [sb(f"gm{i}", [128, 8, 8]) for i in range(2)]
        m8 = [sb(f"m8{i}", [128, 8, 8]) for i in range(2)]
        sel = [sb(f"sel{i}", [128, 8, 8]) for i in range(2)]
        mv = [sb(f"mv{i}", [128, 8, 8], BF16) for i in range(2)]
        carry = [sb(f"carry{i}", [128, 512], BF16) for i in range(2)]
        b31g = sb("b31g_s", [128, 8])
        psall = es.enter_context(nc.psum_tensor("psall", [128, 4096], F32))
        ps = [psall[:, i * 512:(i + 1) * 512] for i in range(8)]
        PSP = lambda b: psall[:, b * 512:(b + 2) * 512].rearrange("p (a f) -> p a f", a=2)

        sems = {}

        def sem(name):
            if name not in sems:
                sems[name] = es.enter_context(nc.semaphore(name))
            return sems[name]

        for e in COMPUTE:
            sem("c_" + e)

        kkeys = [("k", c, t) for c in range(4) for t in range(4)]
        vkeys = [("V", t) for t in range(16)]
        mkeys = [("maskT", t) for t in range(16)]
        P.alias[("ws", 2)] = mkeys
        P.alias[("ws", 3)] = mkeys
        for k in mkeys:
            P.alias[k] = [("ws", 2), ("ws", 3)]
        P.alias["wup"] = kkeys
        P.alias["wout"] = vkeys
        for k in kkeys:
            P.alias[k] = ["wup"]
        for k in vkeys:
            P.alias[k] = ["wout"]

        def cload(dst, src, key):
            P.dma("sp", "const", lambda e, dst=dst, src=src: e.dma_start(out=dst, in_=src), w=[key])

        for dst, src, key in [
            (normw[:], normw_d, "normw"), (qkw[:], qkw_d, "qkw"), (mgb[:], mgb_d, "mgb"),
            (biasD[:], biasD_d, "biasD"), (biasO[:], biasO_d, "biasO"), (b31[:], b31_d, "b31"),
            (cst[:], cst_d, "cst"), (b31g[:], b31g_d, "b31g"), (ident[:], ident_d, "ident"), (bones[:], bones_d, "bones"),
            (tri[:], tri_d, "tri"), (ones[:], ones_d, "ones"), (smask[:], smask_d, "smask"),
            (esel[:], esel_d, "esel"), (cmneg[:], cmneg_d, "cmneg"), (cm01[:], cm01_d, "cm01"),
            (csel[:], csel_d, "csel"), (rsel[:], rsel_d, "rsel"), (negm[:], negm_d, "negm"),
        ]:
            cload(dst, src, key)
        for h in range(8):
            P.op("dve", lambda e, h=h: e.tensor_scalar(out=biasD[:, h, :], in0=biasD[:, h, :], scalar1=b31[:, h:h + 1],
                                                      scalar2=None, op0=ALU.subtract), r=["b31", "biasD"], w=["biasD"])
            P.op("dve", lambda e, h=h: e.tensor_scalar(out=biasO[:, h, :], in0=biasO[:, h, :], scalar1=b31[:, h:h + 1],
                                                      scalar2=None, op0=ALU.subtract), r=["b31", "biasO"], w=["biasO"])

        P.op("pool", lambda e: e.memset(maskT[:], 0.0), w=[("maskT", t) for t in range(16)])
        for i_ in range(2):
            P.op("pool", lambda e, i_=i_: e.memset(carry[i_][:], 0.0), w=[("carry", i_)])
        cnt = {}

        def rr(name, n):
            v = cnt.get(name, 0)
            cnt[name] = v + 1
            return v % n

        def ft():
            return rr("ft", NFT)

        def bt():
            return rr("bt", NBT)

        def ftp():
            return 2 * rr("ftp", NFT // 2)

        def btp():
            return 2 * rr("btp", NBT // 2)

        def pipeline(units, nst):
            n = len(units)
            for t in range(n + nst - 1):
                for st in range(nst):
                    u = t - st
                    if 0 <= u < n and units[u][st] is not None:
                        units[u][st]()

        wseq = []
        for _ in range(nseq):
            wseq += [512, 0, 1024, 2048, 2560, 3072]
            for _t in range(4):
                wseq += [1536, 3584, 4096, 5120, 4608, 5632]
        wstate = {"issued": 0, "use": 0, "mask_free": False}

        def wslot(k):
            j = k % 30
            return [0, 1, 0, 1, 2, 3][j] if j < 6 else (j - 6) % 4

        def wload(col0):
            i = wstate["use"]
            wstate["use"] = i + 1
            assert wseq[i] == col0, (i, wseq[i], col0)
            while wstate["issued"] <= min(i + 3, len(wseq) - 1):
                k = wstate["issued"]
                sl_ = wslot(k)
                if sl_ >= 2 and not wstate["mask_free"]:
                    assert k > i, (k, i)
                    break
                prev = [p for p in range(k) if wslot(p) == sl_]
                if prev and prev[-1] > i - 2 and k > i:
                    break
                src = win_d[:, wseq[k]:wseq[k] + 512].rearrange("(c p) n -> p c n", p=128)
                P.dma("pool", f"w{sl_}", lambda e, sl_=sl_, src=src: e.dma_start(out=wsl[sl_], in_=src), w=[("ws", sl_)])
                wstate["issued"] = k + 1
            return wslot(i)

        def bank(lo=0, hi=8, name=None):
            return lo + rr(name or ("bank%d_%d" % (lo, hi)), hi - lo)

        def seq_body(seq):
            t0 = seq * S

            def ck(k):
                if stop <= k:
                    raise _Stop()
            ck(0)
            wstate["mask_free"] = False
            import os
            LV = int(os.environ.get('P1LV', '9'))
            for tt in range(16 if LV >= 9 else int(os.environ.get('P1N', '3'))):
                s = rr("x", 2)
                src = x_d[t0 + tt * 128: t0 + (tt + 1) * 128, :]
                P.dma("sp", f"x{s}", lambda e, s=s, src=src: e.dma_start(out=xt[s][:], in_=src), w=[("xt", s)])
                P.op("act", lambda e, s=s: e.activation(out=sqj, in_=xt[s][:], func=AF.Square, accum_out=stat[s][:, 0:1]),
                     r=[("xt", s)], w=[("stat", s, 0), ("bt", 0), ("bt", 1)])
                if LV < 2:
                    continue
                P.op("act", lambda e, s=s: e.activation(out=stat[s][:, 1:2], in_=stat[s][:, 0:1], func=AF.Ln,
                                                        scale=1.0 / D, bias=cst[:, 1:2]),
                     r=[("stat", s, 0), "cst"], w=[("stat", s, 1)])
                P.op("act", lambda e, s=s: e.activation(out=stat[s][:, 2:3], in_=stat[s][:, 1:2], func=AF.Exp,
                                                        scale=-0.5, bias=cst[:, 2:3]),
                     r=[("stat", s, 1), "cst"], w=[("stat", s, 2)])
                if LV < 3:
                    continue
                P.op("dve", lambda e, s=s: e.tensor_scalar(out=hbf[s][:], in0=xt[s][:], scalar1=stat[s][:, 2:3],
                                                          scalar2=None, op0=ALU.mult),
                     r=[("xt", s), ("stat", s, 2)], w=["hbf"])
                if LV < 4:
                    continue
                b = bank()
                pv = ps[b][:].bitcast(BF16)

                def f_tr(e, s=s, pv=pv):
                    for dc in range(8):
                        i = e.transpose(pv[:, dc * 128:(dc + 1) * 128], hbf[s][:, dc * 128:(dc + 1) * 128], ident[:])
                    return i
                P.op("pe", f_tr, r=["hbf", "ident"], x=[("ps", b)])

                if LV < 5:
                    continue

                def f_ev(e, tt=tt, pv=pv):
                    return e.tensor_tensor(out=xT[:, :, tt * 128:(tt + 1) * 128],
                                           in0=pv.rearrange("p (c t) -> p c t", c=8),
                                           in1=normw[:, :].unsqueeze(2).broadcast_to([128, 8, 128]), op=ALU.mult)
                P.op("dve", f_ev, r=["normw"], w=[("xT", tt)], x=[("ps", b)])

            ck(1)
            def proj_fm(s, c, tq, lo=0, hi=8):
                b = bank(lo, hi)

                def f(e, s=s, c=c, tq=tq, b=b):
                    for dc in range(8):
                        i = e.matmul(ps[b][:], lhsT=wsl[s][:, dc, c * 128:(c + 1) * 128],
                                     rhs=xT[:, dc, tq * 512:(tq + 1) * 512], start=(dc == 0), stop=(dc == 7))
                    return i
                P.op("pe", f, r=[("ws", s)] + [("xT", tq * 4 + i) for i in range(4)], x=[("ps", b)])
                return b

            def qk_unit(s, c, tq, wcol, fin):
                st_ = {}

                def S0():
                    b = proj_fm(s, c, tq, 0, 4)
                    st_["b"] = b
                    sq_ = bt()
                    st_["sq"] = sq_
                    P.op("act", lambda e: e.activation(out=BT[sq_][:], in_=ps[b][:], func=AF.Square), w=[("bt", sq_)], x=[("ps", b)])

                def S1():
                    sq_ = st_["sq"]
                    b2 = bank(4, 6)
                    P.op("pe", lambda e: e.matmul(ps[b2][:], lhsT=bones[:], rhs=BT[sq_][:], start=True, stop=True),
                         r=[("bt", sq_), "bones"], x=[("ps", b2)])
                    ln_ = ft()
                    P.op("act", lambda e: e.activation(out=FT[ln_][:], in_=ps[b2][:], func=AF.Ln, bias=cst[:, 1:2]),
                         r=["cst"], w=[("ft", ln_)], x=[("ps", b2)])
                    rs_ = ft()
                    st_["rs"] = rs_
                    P.op("act", lambda e: e.activation(out=FT[rs_][:], in_=FT[ln_][:], func=AF.Exp, scale=-0.5, bias=cst[:, 2:3]),
                         r=[("ft", ln_), "cst"], w=[("ft", rs_)])

                def S2():
                    b, rs_ = st_["b"], st_["rs"]
                    h_ = ft()
                    P.op("dve", lambda e: e.scalar_tensor_tensor(out=FT[h_][:], in0=ps[b][:], scalar=qkw[:, wcol:wcol + 1],
                                                               in1=FT[rs_][:], op0=ALU.mult, op1=ALU.mult),
                         r=[("ft", rs_), "qkw"], w=[("ft", h_)], x=[("ps", b)])
                    fin(h_)
                return [S0, S1, S2]

            def proj_v(col0):
                s = wload(col0)
                for tt in range(16):
                    b = bank()

                    def f(e, s=s, tt=tt, b=b):
                        for dc in range(8):
                            i = e.matmul(ps[b][:], lhsT=xT[:, dc, tt * 128:(tt + 1) * 128], rhs=wsl[s][:, dc, :],
                                         start=(dc == 0), stop=(dc == 7))
                        return i
                    P.op("pe", f, r=[("ws", s), ("xT", tt)], x=[("ps", b)])
                    P.op("act", lambda e, tt=tt, b=b: e.activation(out=VV[:, tt, :], in_=ps[b][:], func=AF.Copy),
                         w=[("V", tt)], x=[("ps", b)])

            def proj_plain(col0, dst, mk, scale):
                s = wload(col0)
                for tq in range(4):
                    for c in range(4):
                        b = proj_fm(s, c, tq)
                        P.op("dve", lambda e, c=c, tq=tq, b=b: e.tensor_scalar(out=dst[:, c, tq * 512:(tq + 1) * 512], in0=ps[b][:],
                                                                              scalar1=scale, scalar2=None, op0=ALU.mult),
                             w=mk(c, tq), x=[("ps", b)])

            def dump(nm, src_ap, keys):
                if debug:
                    P.dma("sp", "dbg", lambda e: e.dma_start(out=dbg[nm], in_=src_ap), r=keys)

            qakeys = [("qa", c, q) for c in range(4) for q in range(8)]
            qbkeys = [("qb", c, q) for c in range(4) for q in range(4)]

            s = wload(512)
            units = []
            for tq in range(4):
                for c in range(4):
                    def fin(h_, c=c, tq=tq):
                        P.op("dve", lambda e: e.tensor_copy(out=kk[:, c, tq * 512:(tq + 1) * 512], in_=FT[h_][:]),
                             r=[("ft", h_)], w=[("k", c, tq)])
                        P.op("dve", lambda e: e.tensor_reduce(
                            out=kmean[:, c, tq * 2:(tq + 1) * 2], in_=FT[h_][:].rearrange("p (a b) -> p a b", a=2), axis=AX.X, op=ALU.add),
                            r=[("ft", h_)], w=[("kmean", c, tq)])
                    units.append(qk_unit(s, c, tq, 1, fin))
            pipeline(units, 3)
            ck(2)
            s = wload(0)
            units = []
            for tq in range(4):
                gviews = [ps[6 + hh][:, 0:128].rearrange("p (i c n) -> p i c n", i=4, c=4) for hh in range(2)]
                for c in range(4):
                    def fin(h_, c=c, tq=tq, gviews=gviews):
                        P.op("dve", lambda e: e.tensor_scalar(out=qa[:, c, tq * 512:(tq + 1) * 512], in0=FT[h_][:], scalar1=0.125, scalar2=None, op0=ALU.mult),
                             r=[("ft", h_)], w=[("qa", c, 2 * tq), ("qa", c, 2 * tq + 1)])

                        def f_gate(e):
                            for i in range(4):
                                for hh in range(2):
                                    ins = e.matmul(gviews[hh][:, i, c, :], lhsT=FT[h_][hh * 64:(hh + 1) * 64, i * 128:(i + 1) * 128],
                                                   rhs=kmean[hh * 64:(hh + 1) * 64, c, :], start=True, stop=True)
                            return ins
                        P.op("pe", f_gate, r=[("ft", h_)] + [("kmean", c, t) for t in range(4)], x=[("ps", 6), ("ps", 7)])
                    units.append(qk_unit(s, c, tq, 0, fin))

                def topk(tq=tq, gviews=gviews):
                    for i in range(4):
                        tt = tq * 4 + i
                        qblk = tt // 2
                        sl = rr("g", 2)

                        def f_gm(e, sl=sl, i=i, qblk=qblk, gviews=gviews):
                            for hh in range(2):
                                ins = e.tensor_tensor(out=gm[sl][:, hh * 4:(hh + 1) * 4, :], in0=gviews[hh][:, i, :, :],
                                                      in1=cmneg[:, qblk:qblk + 1, :].broadcast_to([128, 4, 8]), op=ALU.add)
                            return ins
                        P.op("dve", f_gm, r=["cmneg"], w=[("gm", sl)], x=[("ps", 6), ("ps", 7)])

                        def f_m8(e, sl=sl):
                            for h in range(8):
                                ins = e.max(out=m8[sl][:, h, :], in_=gm[sl][:, h, :])
                            return ins
                        P.op("dve", f_m8, r=[("gm", sl)], w=[("m8", sl)])
                        P.op("dve", lambda e, sl=sl: e.tensor_tensor(out=sel[sl][:], in0=gm[sl][:], in1=m8[sl][:, :, 2:3].broadcast_to([128, 8, 8]), op=ALU.is_ge),
                             r=[("gm", sl), ("m8", sl)], w=[("sel", sl)])
                        P.op("dve", lambda e, sl=sl: e.tensor_scalar(out=sel[sl][:], in0=sel[sl][:], scalar1=-1.0, scalar2=-NEG, op0=ALU.add, op1=ALU.mult),
                             r=[("sel", sl)], w=[("sel", sl)])
                        P.op("dve", lambda e, sl=sl, qblk=qblk: e.tensor_tensor(out=sel[sl][:], in0=sel[sl][:], in1=cm01[:, qblk:qblk + 1, :].broadcast_to([128, 8, 8]), op=ALU.mult),
                             r=[("sel", sl), "cm01"], w=[("sel", sl)])
                        P.op("dve", lambda e, sl=sl: e.tensor_tensor(out=mv[sl][:], in0=sel[sl][:], in1=b31g[:, :].unsqueeze(2).broadcast_to([128, 8, 8]), op=ALU.add),
                             r=[("sel", sl), "b31g"], w=[("mv", sl)])
                        b = bank(4, 6)
                        pv = ps[b][:].bitcast(BF16)

                        def f_mt(e, sl=sl, pv=pv):
                            for hh in range(2):
                                for c in range(4):
                                    ins = e.transpose(pv[hh * 64:hh * 64 + 8, c * 128:(c + 1) * 128], mv[sl][:, hh * 4 + c, :], ident[:])
                            return ins
                        P.op("pe", f_mt, r=[("mv", sl), "ident"], x=[("ps", b)])

                        def f_mc(e, tt=tt, pv=pv):
                            for hh in range(2):
                                ins = e.activation(out=maskT[hh * 64:hh * 64 + 8, :, tt * 128:(tt + 1) * 128],
                                                   in_=pv[hh * 64:hh * 64 + 8, 0:512].rearrange("p (c t) -> p c t", c=4), func=AF.Copy)
                            return ins
                        P.op("act", f_mc, w=[("maskT", tt)], x=[("ps", b)])

                units.append([None, None, topk])
            pipeline(units, 3)
            ck(3)
            proj_v(1024)
            dump("qa", qa[:], qakeys)
            dump("ka", kk, kkeys)
            dump("va", VV, vkeys)
            dump("mask", maskT[:], [("maskT", t) for t in range(16)])

            ck(4)
            units = []
            it = 0
            for qblk in range(8):
                for c in range(4):
                    ob_, db_ = (4, 5) if it % 2 == 0 else (6, 7)
                    it += 1
                    for n in range(qblk + 1):
                        own = (n == qblk)
                        st_ = {}

                        def A(c=c, n=n, own=own, qblk=qblk, st_=st_):
                            zb = 2 * rr("mzb", 2)
                            st_["zb"] = zb
                            zp = PSP(zb)

                            def f_qk(e):
                                for hh in range(2):
                                    p0 = hh * 64
                                    zv = zp[:, hh, :].rearrange("p (k q) -> p k q", k=2)
                                    for kt in range(2):
                                        c0 = 128 if (own and kt == 1) else 0
                                        e.matmul(zv[:, kt, c0:256], lhsT=kk[p0:p0 + 64, c, (2 * n + kt) * 128:(2 * n + kt + 1) * 128],
                                                 rhs=qa[p0:p0 + 64, c, qblk * 256 + c0:(qblk + 1) * 256], start=True, stop=False)
                                        ins = e.matmul(zv[:, kt, c0:256], lhsT=esel[p0:p0 + 64, n, :], rhs=maskT[p0:p0 + 64, c, qblk * 256 + c0:(qblk + 1) * 256],
                                                       start=False, stop=True)
                                return ins
                            P.op("pe", f_qk, r=[("k", c, n // 2), ("qa", c, qblk), "esel", ("maskT", 2 * qblk), ("maskT", 2 * qblk + 1)],
                                 x=[("ps", zb), ("ps", zb + 1)])
                            if own or n == qblk - 1:
                                def f_b(e):
                                    for hh in range(2):
                                        h = 2 * c + hh
                                        zv = zp[:, hh, :].rearrange("p (k q) -> p k q", k=2)
                                        if own:
                                            e.tensor_tensor(out=zv[:, 0, 0:128], in0=zv[:, 0, 0:128], in1=biasD[:, h, :], op=ALU.add)
                                            e.tensor_tensor(out=zv[:, 0, 128:256], in0=zv[:, 0, 128:256], in1=biasO[:, h, :], op=ALU.add)
                                            ins = e.tensor_tensor(out=zv[:, 1, 128:256], in0=zv[:, 1, 128:256], in1=biasD[:, h, :], op=ALU.add)
                                        else:
                                            ins = e.tensor_tensor(out=zv[:, 1, 0:128], in0=zv[:, 1, 0:128], in1=biasO[:, h, :], op=ALU.add)
                                    return ins
                                P.op("dve", f_b, r=["biasD", "biasO"], x=[("ps", zb), ("ps", zb + 1)])
                            pp = btp()
                            st_["pp"] = pp
                            P.op("act", lambda e: e.activation(out=BTP(pp), in_=zp, func=AF.Exp, bias=cst[:, 2:3]),
                                 r=["cst"], w=[("bt", pp), ("bt", pp + 1)], x=[("ps", zb), ("ps", zb + 1)])

                        def B(c=c, n=n, own=own, qblk=qblk, st_=st_, ob_=ob_, db_=db_):
                            pp = st_["pp"]

                            def f_av(e):
                                for hh in range(2):
                                    p0 = hh * 64
                                    h = 2 * c + hh
                                    pvv = BTP(pp)[:, hh, :].rearrange("p (k q) -> p k q", k=2)
                                    for kt in range(2):
                                        c0 = 128 if (own and kt == 1) else 0
                                        first = (n == 0 and kt == 0)
                                        last = (own and kt == 1)
                                        e.matmul(ps[ob_][p0:p0 + 64, c0:256], lhsT=VV[:, 2 * n + kt, h * 64:(h + 1) * 64], rhs=pvv[:, kt, c0:256],
                                                 start=first, stop=last, skip_group_check=True)
                                        ins = e.matmul(ps[db_][p0:p0 + 64, c0:256], lhsT=ones[:, 0:64], rhs=pvv[:, kt, c0:256],
                                                       start=first, stop=last, skip_group_check=True)
                                return ins
                            P.op("pe", f_av, r=[("bt", pp), ("bt", pp + 1), ("V", 2 * n), ("V", 2 * n + 1), "ones"], x=[("ps", ob_), ("ps", db_)])
                            if own:
                                rd_ = ft()
                                P.op("dve", lambda e: e.reciprocal(out=FT[rd_][:, 0:256], in_=ps[db_][:, 0:256]), w=[("ft", rd_)], x=[("ps", db_)])
                                P.op("dve", lambda e: e.tensor_tensor(out=qa[:, c, qblk * 256:(qblk + 1) * 256], in0=ps[ob_][:, 0:256],
                                                                      in1=FT[rd_][:, 0:256], op=ALU.mult),
                                     r=[("ft", rd_)], w=[("qa", c, qblk)], x=[("ps", ob_)])
                        units.append([A, B])
            units = [[u_[0], None, u_[1]] for u_ in units]
            pipeline(units, 3)
            wstate["mask_free"] = True
            dump("oa", qa[:], qakeys)

            ck(5)
            proj_plain(2048, qb_, lambda c, tq: [("qb", c, tq)], 0.125)
            proj_plain(2560, kk, lambda c, tq: [("k", c, tq)], 1.0)
            proj_v(3072)
            dump("qb", qb_[:], qbkeys)
            dump("kb", kk, kkeys)
            dump("vb", VV, vkeys)

            ck(6)
            units = []
            for qc in range(4):
                for c in range(4):
                    ob_ = 6
                    nkb = 4 * qc + 4
                    for u in range(nkb):
                        kbk = nkb - 1 - u
                        j = kbk - 4 * qc
                        c0 = j * 128 if j >= 0 else 0
                        st_ = {}

                        def A(c=c, qc=qc, u=u, kbk=kbk, j=j, c0=c0, st_=st_):
                            zb = 2 * rr("szb", 2)
                            zp = PSP(zb)

                            def f_z(e):
                                for hh in range(2):
                                    p0 = hh * 64
                                    ins = e.matmul(zp[:, hh, c0:512], lhsT=kk[p0:p0 + 64, c, kbk * 128:(kbk + 1) * 128],
                                                   rhs=qb_[p0:p0 + 64, c, qc * 512 + c0:(qc + 1) * 512], start=True, stop=(j < 0))
                                    if j >= 0:
                                        ins = e.matmul(zp[:, hh, c0:c0 + 128], lhsT=ident[:], rhs=negm[:], start=False, stop=True)
                                return ins
                            P.op("pe", f_z, r=[("k", c, kbk // 4), ("qb", c, qc), "ident", "negm"], x=[("ps", zb), ("ps", zb + 1)])
                            e_ = 2 * rr('sbE', 3)
                            st_["e"] = e_
                            ekeys = [("ft", e_), ("ft", e_ + 1)]
                            P.op("act", lambda e: e.activation(out=FTP(e_)[:, :, c0:512], in_=zp[:, :, c0:512], func=AF.Exp, bias=cst[:, 2:3]),
                                 r=["cst"], w=ekeys, x=[("ps", zb), ("ps", zb + 1)])
                            L_ = btp()
                            st_["L"] = L_
                            P.op("act", lambda e: e.activation(out=BTP(L_)[:, :, c0:512], in_=FTP(e_)[:, :, c0:512], func=AF.Ln, bias=cst[:, 0:1]),
                                 r=ekeys + ["cst"], w=[("bt", L_), ("bt", L_ + 1)])

                        def B(c=c, qc=qc, u=u, kbk=kbk, c0=c0, st_=st_, nkb=nkb):
                            e_, L_ = st_["e"], st_["L"]
                            cp = PSP(4)
                            cprev, ccur = (u + 1) % 2, u % 2
                            if u == 0:
                                for i_ in range(2):
                                    P.op("dve", lambda e, i_=i_: e.memset(carry[i_][0:33, :], 0.0), w=[("carry", i_)])

                            def f_cs(e):
                                for hh in range(2):
                                    if u < nkb - 1:
                                        e.matmul(ps[7][:, c0:512], lhsT=csel[:, hh, :], rhs=BTP(L_)[:, hh, c0:512],
                                                 start=(u == 0 and hh == 0), stop=False, skip_group_check=True)
                                    ins = e.matmul(cp[:, hh, c0:512], lhsT=tri[:], rhs=BTP(L_)[:, hh, c0:512], start=True, stop=(u == 0))
                                    if u > 0:
                                        ins = e.matmul(cp[:, hh, c0:512], lhsT=rsel[:, hh, :], rhs=carry[cprev][:, c0:512], start=False, stop=True)
                                return ins
                            P.op("pe", f_cs, r=[("bt", L_), ("bt", L_ + 1), ("carry", cprev), "tri", "csel", "rsel"], x=[("ps", 4), ("ps", 5), ("ps", 7)])
                            if u < nkb - 1:
                                P.op("dve", lambda e: e.tensor_copy(out=carry[ccur][0:33, c0:512], in_=ps[7][0:33, c0:512]),
                                     w=[("carry", ccur)], x=[("ps", 7)])
                            g_ = 6 + 2 * rr('sbG', 2)
                            gkeys = [("ft", g_), ("ft", g_ + 1)]
                            P.op("act", lambda e: e.activation(out=FTP(g_)[:, :, c0:512], in_=cp[:, :, c0:512], func=AF.Exp, scale=-1.0, bias=cst[:, 2:3]),
                                 r=["cst"], w=gkeys, x=[("ps", 4), ("ps", 5)])
                            w_ = btp()
                            st_["w"] = w_
                            P.op("dve", lambda e: e.tensor_tensor(out=BTP(w_)[:, :, c0:512], in0=FTP(e_)[:, :, c0:512], in1=FTP(g_)[:, :, c0:512], op=ALU.mult),
                                 r=[("ft", e_), ("ft", e_ + 1)] + gkeys, w=[("bt", w_), ("bt", w_ + 1)])

                        def C(c=c, qc=qc, u=u, kbk=kbk, c0=c0, st_=st_, nkb=nkb, ob_=ob_):
                            w_ = st_["w"]

                            def f_av(e):
                                for hh in range(2):
                                    p0 = hh * 64
                                    h = 2 * c + hh
                                    ins = e.matmul(ps[ob_][p0:p0 + 64, c0:512], lhsT=VV[:, kbk, h * 64:(h + 1) * 64], rhs=BTP(w_)[:, hh, c0:512],
                                                   start=(u == 0), stop=(u == nkb - 1), skip_group_check=True)
                                return ins
                            P.op("pe", f_av, r=[("bt", w_), ("bt", w_ + 1), ("V", kbk)], x=[("ps", ob_)])
                            if u == nkb - 1:
                                P.op("dve", lambda e: e.tensor_copy(out=qb_[:, c, qc * 512:(qc + 1) * 512], in_=ps[ob_][:]),
                                     w=[("qb", c, qc)], x=[("ps", ob_)])
                        units.append([A, B, C])
            pipeline(units, 3)
            dump("ob", qb_[:], qbkeys)

            ck(7)
            P.dma("pool", "w6", lambda e: e.dma_start(out=wupa, in_=wua_d.rearrange("(c p) n -> p c n", p=128)), w=["wup"])
            P.dma("pool", "w6", lambda e: e.dma_start(out=wupb, in_=wub_d.rearrange("(c p) n -> p c n", p=128)), w=["wup"])
            P.dma("pool", "w6", lambda e: e.dma_start(out=wout, in_=wo_d.rearrange("(c p) n -> p c n", p=128)), w=["wout"])
            w6keys = ["wup", "wout"]
            units = []
            for tq in range(4):
                xkeys = [("xT", tq * 4 + i) for i in range(4)]
                for br, (zcol, Osrc) in enumerate(((1536, qa), (3584, qb_))):
                    slot_ = {}
                    for c in range(4):
                        st_ = {}

                        def S0(br=br, zcol=zcol, c=c, tq=tq, st_=st_, slot_=slot_):
                            if c == 0:
                                slot_["s"] = wload(zcol)
                            st_["b"] = proj_fm(slot_["s"], c, tq)

                        def S1(br=br, Osrc=Osrc, c=c, tq=tq, st_=st_):
                            b = st_["b"]
                            sg_ = 4 + rr('ft6', 4)
                            P.op("act", lambda e: e.activation(out=FT[sg_][:], in_=ps[b][:], func=AF.Sigmoid, bias=cst[:, 2:3]),
                                 r=["cst"], w=[("ft", sg_)], x=[("ps", b)])
                            t_ = 4 + rr('ft6', 4)
                            P.op("dve", lambda e: e.tensor_tensor(out=FT[t_][:], in0=ps[b][:], in1=FT[sg_][:], op=ALU.mult),
                                 r=[("ft", sg_)], w=[("ft", t_)], x=[("ps", b)])
                            okeys = [("qa", c, 2 * tq), ("qa", c, 2 * tq + 1)] if br == 0 else [("qb", c, tq)]
                            P.op("dve", lambda e: e.tensor_tensor(out=Gt[br][:, c, :], in0=FT[t_][:],
                                                                  in1=Osrc[:, c, tq * 512:(tq + 1) * 512], op=ALU.mult),
                                 r=[("ft", t_)] + okeys, w=[("bt", br * 4 + c)])
                        units.append([S0, S1])
                units.append([None, None])
                for half in range(2):
                    slot_ = {}
                    for f4 in range(4):
                        f = half * 4 + f4
                        st_ = {}

                        def S0(half=half, f4=f4, f=f, tq=tq, st_=st_, slot_=slot_, xkeys=xkeys):
                            if f4 == 0:
                                slot_["a"] = wload(4096 + half * 512)
                                slot_["b"] = wload(5120 + half * 512)
                            bya, byb, bga, bgb = bank(), bank(), bank(), bank()
                            st_["banks"] = (bya, byb, bga, bgb)
                            for br, (bg, sgw) in enumerate(((bga, slot_["a"]), (bgb, slot_["b"]))):
                                def f_g(e, bg=bg, sgw=sgw):
                                    for dc in range(8):
                                        ins = e.matmul(ps[bg][:], lhsT=wsl[sgw][:, dc, f4 * 128:(f4 + 1) * 128],
                                                       rhs=xT[:, dc, tq * 512:(tq + 1) * 512], start=(dc == 0), stop=(dc == 7))
                                    return ins
                                P.op("pe", f_g, r=[("ws", sgw)] + xkeys, x=[("ps", bg)])
                            for br, (by, wup_) in enumerate(((bya, wupa), (byb, wupb))):
                                def f_up(e, by=by, wup_=wup_, br=br):
                                    for kc in range(4):
                                        ins = e.matmul(ps[by][:], lhsT=wup_[:, kc, f * 128:(f + 1) * 128], rhs=Gt[br][:, kc, :], start=(kc == 0), stop=(kc == 3))
                                    return ins
                                P.op("pe", f_up, r=w6keys + [("bt", br * 4 + kc) for kc in range(4)], x=[("ps", by)])

                        def S1(f=f, st_=st_):
                            bya, byb, bga, bgb = st_["banks"]
                            tms = []
                            for br, (bg, by) in enumerate(((bga, bya), (bgb, byb))):
                                sg_ = 4 + rr('ft6', 4)
                                P.op("act", lambda e, sg_=sg_, bg=bg, br=br: e.activation(out=FT[sg_][:], in_=ps[bg][:], func=AF.Sigmoid, bias=mgb[:, br * 8 + f: br * 8 + f + 1]),
                                     r=["mgb"], w=[("ft", sg_)], x=[("ps", bg)])
                                tm_ = 4 + rr('ft6', 4)
                                tms.append(tm_)
                                P.op("dve", lambda e, sg_=sg_, tm_=tm_, by=by: e.tensor_tensor(out=FT[tm_][:], in0=ps[by][:], in1=FT[sg_][:], op=ALU.mult),
                                     r=[("ft", sg_)], w=[("ft", tm_)], x=[("ps", by)])
                            P.op("dve", lambda e, ta=tms[0], tb=tms[1]: e.tensor_tensor(out=yT[:, f, :], in0=FT[ta][:], in1=FT[tb][:], op=ALU.add),
                                 r=[("ft", tms[0]), ("ft", tms[1])], w=[("ft", f // 2)])
                        units.append([S0, S1])
                units.append([None, None])
                for i in range(4):
                    tt = tq * 4 + i
                    xs_ = {}
                    for half in range(2):
                        st_ = {}

                        def S0(i=i, tt=tt, half=half, st_=st_, xs_=xs_):
                            if half == 0:
                                s = rr("x", 2)
                                xs_["s"] = s
                                src = x_d[t0 + tt * 128: t0 + (tt + 1) * 128, :]
                                P.dma("sp", f"x{s}", lambda e: e.dma_start(out=xt[s][:], in_=src), w=[("xt", s)])
                            b = bank()
                            st_["b"] = b

                            def f_o(e):
                                for fc in range(8):
                                    ins = e.matmul(ps[b][:], lhsT=yT[:, fc, i * 128:(i + 1) * 128], rhs=wout[:, fc, half * 512:(half + 1) * 512],
                                                   start=(fc == 0), stop=(fc == 7))
                                return ins
                            P.op("pe", f_o, r=w6keys + [("ft", fc) for fc in range(4)], x=[("ps", b)])

                        def S1(tt=tt, half=half, st_=st_, xs_=xs_):
                            b, s = st_["b"], xs_["s"]
                            P.op("dve", lambda e: e.tensor_tensor(out=xt[s][:, half * 512:(half + 1) * 512], in0=ps[b][:],
                                                                  in1=xt[s][:, half * 512:(half + 1) * 512], op=ALU.add),
                                 r=[("xt", s)], w=[("xt", s)], x=[("ps", b)])
                            if half == 1:
                                dst = out_d[t0 + tt * 128: t0 + (tt + 1) * 128, :]
                                P.dma("sp", f"st{s}", lambda e: e.dma_start(out=dst, in_=xt[s][:]), r=[("xt", s)])
                        units.append([S0, S1])
            pipeline(units, 2)

        try:
            for seq in range(nseq):
                seq_body(seq)
        except _Stop:
            pass
        print('ops', {k: len(v) for k, v in P.ops.items()})
        block = es.enter_context(nc.Block())
        for en in COMPUTE:
            k = 0
            for op in P.ops[en]:
                if op.signal and op.dma is None:
                    k += 1
                    op.sig_idx = k

        def emit(en, eh):
            for op in P.ops[en]:
                for key, val in op.waits.items():
                    if isinstance(key, tuple):
                        eh.wait_ge(sem("d_" + key[1]), val)
                    else:
                        eh.wait_ge(sem("c_" + key), P.ops[key][val - 1].sig_idx)
                ins = op.fn(eh)
                if op.dma is not None:
                    ins.then_inc(sem("d_" + op.dma[0]), 16)
                elif op.signal:
                    ins.then_inc(sem("c_" + en), 1)
            if en == "sp":
                for name, c in P.dma_count.items():
                    if name.startswith("st") or name == "dbg":
                        eh.wait_ge(sem("d_" + name), c)

        for name in P.dma_count:
            sem("d_" + name)

        @block.tensor
        def _(e):
            emit("pe", e)

        @block.scalar
        def _(e):
            emit("act", e)

        @block.vector
        def _(e):
            emit("dve", e)

        @block.gpsimd
        def _(e):
            emit("pool", e)

        @block.sync
        def _(e):
            emit("sp", e)
    return nc


_CONST = None


def _consts():
    global _CONST
    if _CONST is None:
        bf = ml_dtypes.bfloat16
        i = np.arange(128)
        c = {}
        c["ident"] = np.eye(128, dtype=np.float32).astype(bf)
        c["bones"] = ((i[:, None] // 64 == i[None, :] // 64).astype(np.float32) / 64.0).astype(bf)
        c["tri"] = (i[:, None] >= i[None, :]).astype(np.float32).astype(bf)
        c["ones"] = np.ones((128, 128), np.float32).astype(bf)
        c["smask"] = (i[:, None] < i[None, :]).astype(np.float32)
        es = np.zeros((128, 8, 128), np.float32)
        for n in range(8):
            es[n, n, :] = 1.0
            es[64 + n, n, :] = 1.0
        c["esel"] = es.astype(bf)
        n = np.arange(8)
        cm = (n[None, :] < n[:, None]).astype(np.float32)
        c["cm01"] = np.ascontiguousarray(np.broadcast_to(cm[None], (128, 8, 8))).astype(np.float32)
        c["cmneg"] = np.ascontiguousarray(np.broadcast_to(((1.0 - cm) * -1e30)[None], (128, 8, 8))).astype(np.float32)
        cs_ = np.zeros((128, 2, 128), np.float32)
        rs_ = np.zeros((128, 2, 128), np.float32)
        for hh in range(2):
            cs_[:, hh, 32 * hh] = 1.0
            rs_[32 * hh, hh, :] = 1.0
        c["csel"] = cs_.astype(bf)
        c["rsel"] = rs_.astype(bf)
        c["negm"] = np.where(i[:, None] >= i[None, :], NEG, 0.0).astype(np.float32).astype(bf)
        cst = np.zeros((128, 4), np.float32)
        cst[:, 0] = 1.0
        cst[:, 1] = 1e-6
        c["cst"] = cst
        bk = _bucket_table(512)
        d = i[None, :] - i[:, None]
        c["idxD"] = bk[np.maximum(d, 0)]
        c["mskD"] = d >= 0
        c["idxO"] = bk[128 + d]
        _CONST = c
    return _CONST


def _prep(inputs):
    c = _consts()
    f32 = np.float32
    rb = np.asarray(inputs["rel_bias"], f32)
    m = {
        "w_in": np.ascontiguousarray(np.asarray(inputs["w_in"], f32)[0]),
        "w_up_a": np.ascontiguousarray(np.asarray(inputs["w_up_moba"], f32)[0]),
        "w_up_b": np.ascontiguousarray(np.asarray(inputs["w_up_sb"], f32)[0]),
        "w_out": np.ascontiguousarray(np.asarray(inputs["w_out"], f32)[0]),
        "normw": np.ascontiguousarray(np.asarray(inputs["norm_w"], f32)[0].reshape(8, 128).T),
        "qkw": np.ascontiguousarray(np.stack([np.tile(np.asarray(inputs["q_norm_w"], f32)[0], 2),
                                              np.tile(np.asarray(inputs["k_norm_w"], f32)[0], 2)], axis=1)),
        "mgb": np.ascontiguousarray(np.asarray(inputs["merge_gate_b"], f32)[0].reshape(16, 128).T),
        "b31": np.ascontiguousarray(np.broadcast_to(rb[None, :, 31], (128, 8))),
        "b31g": np.ascontiguousarray(np.broadcast_to(rb[None, [0, 2, 4, 6, 1, 3, 5, 7], 31], (128, 8))),
    }
    bd = rb[:, c["idxD"]]
    bd = np.where(c["mskD"][None], bd, f32(NEG))
    m["biasD"] = np.ascontiguousarray(bd.transpose(1, 0, 2)).astype(f32)
    m["biasO"] = np.ascontiguousarray(rb[:, c["idxO"]].transpose(1, 0, 2)).astype(f32)
    for k in ("cst", "ident", "bones", "tri", "ones", "smask", "esel", "cmneg", "cm01", "csel", "rsel", "negm"):
        m[k] = c[k]
    return m


_NC = {}


def kernel(**inputs):
    x = np.asarray(inputs["x"], np.float32)
    B = x.shape[0]
    ncores = 8
    nseq = B // ncores
    shared = _prep(inputs)
    if nseq not in _NC:
        _NC[nseq] = build(nseq)
    nc = _NC[nseq]
    in_maps = []
    for i in range(ncores):
        m = dict(shared)
        m["x"] = np.ascontiguousarray(x[i * nseq:(i + 1) * nseq].reshape(nseq * S, D))
        in_maps.append(m)
    res = run_bass_kernel_spmd(nc, in_maps, core_ids=list(range(ncores)))
    out = np.stack([np.asarray(r["out"], np.float32).reshape(nseq, S, D) for r in res.results], axis=0)
    return out.reshape(B, S, D)
```

```python
import numpy as np
import ml_dtypes
import concourse.bass as bass
import concourse.mybir as mybir
from concourse.bass_utils import run_bass_kernel_spmd

F32 = mybir.dt.float32
BF16 = mybir.dt.bfloat16
AF = mybir.ActivationFunctionType
ALU = mybir.AluOpType
AX = mybir.AxisListType

S = 2048
D = 1024
NEG = -30000.0
COMPUTE = ("pe", "act", "dve", "pool")


class Op:
    __slots__ = ("eng", "fn", "seq", "waits", "signal", "dma", "sig_idx")


class Prog:
    def __init__(self):
        self.ops = {e: [] for e in COMPUTE + ("sp",)}
        self.last_w = {}
        self.readers = {}
        self.alias = {}
        self.known = {e: {} for e in COMPUTE + ("sp",)}
        self.dma_count = {}

    def _dep(self, op, prod):
        if prod is op:
            return
        if prod.dma is not None:
            key, val = ("dma", prod.dma[0]), self.dma_count[prod.dma[0]]
        else:
            if prod.eng == op.eng and op.dma is None and op.eng == "pe":
                return
            key, val = prod.eng, prod.seq
            prod.signal = True
        if self.known[op.eng].get(key, 0) >= val:
            return
        if op.waits.get(key, 0) < val:
            op.waits[key] = val

    def _add(self, eng, fn, r, w, x, dma):
        op = Op()
        op.eng, op.fn, op.waits, op.signal, op.dma, op.sig_idx = eng, fn, {}, False, dma, 0
        lst = self.ops[eng]
        op.seq = len(lst) + 1
        wkeys = []
        for k in tuple(w) + tuple(x):
            wkeys.append(k)
            for a in self.alias.get(k, ()):
                wkeys.append(a)
        for k in r:
            p = self.last_w.get(k)
            if p is not None:
                if p.eng == eng and p.dma is None and dma is None and eng != "pe":
                    p.signal = True
                    if self.known[eng].get(eng, 0) < p.seq and op.waits.get(eng, 0) < p.seq:
                        op.waits[eng] = p.seq
                else:
                    self._dep(op, p)
        for k in wkeys:
            p = self.last_w.get(k)
            if p is not None:
                self._dep(op, p)
            for q in self.readers.get(k, ()):
                self._dep(op, q)
        for k in x:
            pass
        for k in r:
            self.readers.setdefault(k, []).append(op)
        for k in wkeys:
            self.last_w[k] = op
            self.readers[k] = []
        for key, val in op.waits.items():
            if self.known[eng].get(key, 0) < val:
                self.known[eng][key] = val
        lst.append(op)
        return op

    def op(self, eng, fn, r=(), w=(), x=()):
        return self._add(eng, fn, r, w, x, None)

    def dma(self, eng, semname, fn, r=(), w=()):
        c = self.dma_count.get(semname, 0) + 16
        self.dma_count.setdefault(semname, 0)
        op = self._add(eng, fn, r, w, (), (semname, c))
        self.dma_count[semname] = c
        return op


def _bucket_table(maxd):
    n = np.arange(maxd, dtype=np.int64)
    nf = np.maximum(n, 1).astype(np.float32)
    large = 16 + (np.log(nf / np.float32(16)) / np.float32(np.log(128 / 16)) * np.float32(16)).astype(np.int32)
    large = np.minimum(large, 31)
    return np.where(n < 16, n, large).astype(np.int64)


class _Stop(Exception):
    pass


def build(nseq, debug=False, stop=99):
    nc = bass.Bass("TRN2", target_bir_lowering=False)
    P = Prog()

    def din(name, shape, dt=F32):
        return nc.dram_tensor(name, list(shape), dt, kind="ExternalInput").ap()

    x_d = din("x", [nseq * S, D])
    win_d = din("w_in", [D, 6144])
    wua_d = din("w_up_a", [512, D])
    wub_d = din("w_up_b", [512, D])
    wo_d = din("w_out", [D, D])
    normw_d = din("normw", [128, 8])
    qkw_d = din("qkw", [128, 2])
    mgb_d = din("mgb", [128, 16])
    biasD_d = din("biasD", [128, 8, 128])
    biasO_d = din("biasO", [128, 8, 128])
    b31_d = din("b31", [128, 8])
    b31g_d = din("b31g", [128, 8])
    cst_d = din("cst", [128, 4])
    ident_d = din("ident", [128, 128], BF16)
    bones_d = din("bones", [128, 128], BF16)
    tri_d = din("tri", [128, 128], BF16)
    ones_d = din("ones", [128, 128], BF16)
    smask_d = din("smask", [128, 128])
    esel_d = din("esel", [128, 8, 128], BF16)
    cmneg_d = din("cmneg", [128, 8, 8])
    cm01_d = din("cm01", [128, 8, 8])
    csel_d = din("csel", [128, 2, 128], BF16)
    rsel_d = din("rsel", [128, 2, 128], BF16)
    negm_d = din("negm", [128, 128], BF16)
    out_d = nc.dram_tensor("out", [nseq * S, D], F32, kind="ExternalOutput").ap()
    dbg = {}
    if debug:
        for nm in ("qa", "ka", "qb", "kb", "oa", "ob"):
            dbg[nm] = nc.dram_tensor("dbg_" + nm, [128, 4, S], BF16, kind="ExternalOutput").ap()
        for nm in ("va", "vb"):
            dbg[nm] = nc.dram_tensor("dbg_" + nm, [128, 16, 512], BF16, kind="ExternalOutput").ap()
        dbg["mask"] = nc.dram_tensor("dbg_mask", [128, 4, S], BF16, kind="ExternalOutput").ap()

    from contextlib import ExitStack
    es = ExitStack()
    with es:
        def sb(name, shape, dt=F32):
            return es.enter_context(nc.sbuf_tensor(name, list(shape), dt))

        xT = sb("xT", [128, 8, S], BF16)
        qa = sb("qa", [128, 4, S], BF16)
        qb_ = sb("qb", [128, 4, S], BF16)
        R = sb("R", [128, 16384], BF16)
        kk = R[:, 0:8192].rearrange("p (c t) -> p c t", c=4)
        VV = R[:, 8192:16384].rearrange("p (t f) -> p t f", t=16)
        wupa = R[:, 0:4096].rearrange("p (c n) -> p c n", c=4)
        wupb = R[:, 4096:8192].rearrange("p (c n) -> p c n", c=4)
        wout = R[:, 8192:16384].rearrange("p (c n) -> p c n", c=8)
        NWS = 2
        wsl = [sb(f"wsl{i}", [128, 8, 512], BF16)[:] for i in range(NWS)]
        maskT = sb("maskT", [128, 4, S], BF16)
        _mflat = maskT[:].rearrange("p c t -> p (c t)")
        wsl = wsl + [_mflat[:, i * 4096:(i + 1) * 4096].rearrange("p (c n) -> p c n", c=8) for i in range(2)]
        normw = sb("normw_s", [128, 8])
        qkw = sb("qkw_s", [128, 2])
        mgb = sb("mgb_s", [128, 16])
        biasD = sb("biasD_s", [128, 8, 128])
        biasO = sb("biasO_s", [128, 8, 128])
        b31 = sb("b31_s", [128, 8])
        cst = sb("cst_s", [128, 4])
        ident = sb("ident_s", [128, 128], BF16)
        bones = sb("bones_s", [128, 128], BF16)
        tri = sb("tri_s", [128, 128], BF16)
        ones = sb("ones_s", [128, 128], BF16)
        smask = sb("smask_s", [128, 128])
        esel = sb("esel_s", [128, 8, 128], BF16)
        cmneg = sb("cmneg_s", [128, 8, 8])
        cm01 = sb("cm01_s", [128, 8, 8])
        csel = sb("csel_s", [128, 2, 128], BF16)
        rsel = sb("rsel_s", [128, 2, 128], BF16)
        negm = sb("negm_s", [128, 128], BF16)
        xt = [sb(f"xt{i}", [128, D]) for i in range(2)]
        hbf = [sb("hbf0", [128, D], BF16)] * 2
        stat = [sb(f"stat{i}", [128, 4]) for i in range(2)]
        NFT, NBT = 10, 8
        FTall = sb("ftall", [128, NFT * 512])
        BTall = sb("btall", [128, NBT * 512], BF16)
        FT = [FTall[:, i * 512:(i + 1) * 512] for i in range(NFT)]
        BT = [BTall[:, i * 512:(i + 1) * 512] for i in range(NBT)]
        sqj = BTall[:, 0:1024]
        Gt = [BTall[:, br * 2048:(br + 1) * 2048].rearrange("p (c t) -> p c t", c=4) for br in range(2)]
        yT = FTall[:, 0:2048].bitcast(BF16).rearrange("p (f t) -> p f t", f=8)
        FTP = lambda i: FTall[:, i * 512:(i + 2) * 512].rearrange("p (a f) -> p a f", a=2)
        BTP = lambda i: BTall[:, i * 512:(i + 2) * 512].rearrange("p (a f) -> p a f", a=2)
        kmean = sb("kmean", [128, 4, 8])
        gm = [sb(f"gm{i}", [128, 8, 8]) for i in range(2)]
        m8 = [sb(f"m8{i}", [128, 8, 8]) for i in range(2)]
        sel = [sb(f"sel{i}", [128, 8, 8]) for i in range(2)]
        mv = [sb(f"mv{i}", [128, 8, 8], BF16) for i in range(2)]
        carry = [sb(f"carry{i}", [128, 512], BF16) for i in range(2)]
        b31g = sb("b31g_s", [128, 8])
        psall = es.enter_context(nc.psum_tensor("psall", [128, 4096], F32))
        ps = [psall[:, i * 512:(i + 1) * 512] for i in range(8)]
        PSP = lambda b: psall[:, b * 512:(b + 2) * 512].rearrange("p (a f) -> p a f", a=2)

        sems = {}

        def sem(name):
            if name not in sems:
                sems[name] = es.enter_context(nc.semaphore(name))
            return sems[name]

        for e in COMPUTE:
            sem("c_" + e)

        kkeys = [("k", c, t) for c in range(4) for t in range(4)]
        vkeys = [("V", t) for t in range(16)]
        mkeys = [("maskT", t) for t in range(16)]
        P.alias[("ws", 2)] = mkeys
        P.alias[("ws", 3)] = mkeys
        for k in mkeys:
            P.alias[k] = [("ws", 2), ("ws", 3)]
        P.alias["wup"] = kkeys
        P.alias["wout"] = vkeys
        for k in kkeys:
            P.alias[k] = ["wup"]
        for k in vkeys:
            P.alias[k] = ["wout"]

        def cload(dst, src, key):
            P.dma("sp", "const", lambda e, dst=dst, src=src: e.dma_start(out=dst, in_=src), w=[key])

        for dst, src, key in [
            (normw[:], normw_d, "normw"), (qkw[:], qkw_d, "qkw"), (mgb[:], mgb_d, "mgb"),
            (biasD[:], biasD_d, "biasD"), (biasO[:], biasO_d, "biasO"), (b31[:], b31_d, "b31"),
            (cst[:], cst_d, "cst"), (b31g[:], b31g_d, "b31g"), (ident[:], ident_d, "ident"), (bones[:], bones_d, "bones"),
            (tri[:], tri_d, "tri"), (ones[:], ones_d, "ones"), (smask[:], smask_d, "smask"),
            (esel[:], esel_d, "esel"), (cmneg[:], cmneg_d, "cmneg"), (cm01[:], cm01_d, "cm01"),
            (csel[:], csel_d, "csel"), (rsel[:], rsel_d, "rsel"), (negm[:], negm_d, "negm"),
        ]:
            cload(dst, src, key)
        for h in range(8):
            P.op("dve", lambda e, h=h: e.tensor_scalar(out=biasD[:, h, :], in0=biasD[:, h, :], scalar1=b31[:, h:h + 1],
                                                      scalar2=None, op0=ALU.subtract), r=["b31", "biasD"], w=["biasD"])
            P.op("dve", lambda e, h=h: e.tensor_scalar(out=biasO[:, h, :], in0=biasO[:, h, :], scalar1=b31[:, h:h + 1],
                                                      scalar2=None, op0=ALU.subtract), r=["b31", "biasO"], w=["biasO"])

        P.op("pool", lambda e: e.memset(maskT[:], 0.0), w=[("maskT", t) for t in range(16)])
        for i_ in range(2):
            P.op("pool", lambda e, i_=i_: e.memset(carry[i_][:], 0.0), w=[("carry", i_)])
        cnt = {}

        def rr(name, n):
            v = cnt.get(name, 0)
            cnt[name] = v + 1
            return v % n

        def ft():
            return rr("ft", NFT)

        def bt():
            return rr("bt", NBT)

        def ftp():
            return 2 * rr("ftp", NFT // 2)

        def btp():
            return 2 * rr("btp", NBT // 2)

        def pipeline(units, nst):
            n = len(units)
            for t in range(n + nst - 1):
                for st in range(nst):
                    u = t - st
                    if 0 <= u < n and units[u][st] is not None:
                        units[u][st]()

        wseq = []
        for _ in range(nseq):
            wseq += [512, 0, 1024, 2048, 2560, 3072]
            for _t in range(4):
                wseq += [1536, 3584, 4096, 5120, 4608, 5632]
        wstate = {"issued": 0, "use": 0, "mask_free": False}

        def wslot(k):
            j = k % 30
            return [0, 1, 0, 1, 2, 3][j] if j < 6 else (j - 6) % 4

        def wload(col0):
            i = wstate["use"]
            wstate["use"] = i + 1
            assert wseq[i] == col0, (i, wseq[i], col0)
            while wstate["issued"] <= min(i + 3, len(wseq) - 1):
                k = wstate["issued"]
                sl_ = wslot(k)
                if sl_ >= 2 and not wstate["mask_free"]:
                    assert k > i, (k, i)
                    break
                prev = [p for p in range(k) if wslot(p) == sl_]
                if prev and prev[-1] > i - 2 and k > i:
                    break
                src = win_d[:, wseq[k]:wseq[k] + 512].rearrange("(c p) n -> p c n", p=128)
                P.dma("pool", f"w{sl_}", lambda e, sl_=sl_, src=src: e.dma_start(out=wsl[sl_], in_=src), w=[("ws", sl_)])
                wstate["issued"] = k + 1
            return wslot(i)

        def bank(lo=0, hi=8, name=None):
            return lo + rr(name or ("bank%d_%d" % (lo, hi)), hi - lo)

        def seq_body(seq):
            t0 = seq * S

            def ck(k):
                if stop <= k:
                    raise _Stop()
            ck(0)
            wstate["mask_free"] = False
            units = []
            for tt in range(16):
                st_ = {}

                def S0(tt=tt, st_=st_):
                    s = rr("x", 2)
                    st_["s"] = s
                    src = x_d[t0 + tt * 128: t0 + (tt + 1) * 128, :]
                    P.dma("sp", f"x{s}", lambda e: e.dma_start(out=xt[s][:], in_=src), w=[("xt", s)])
                    P.op("act", lambda e: e.activation(out=sqj, in_=xt[s][:], func=AF.Square, accum_out=stat[s][:, 0:1]),
                         r=[("xt", s)], w=[("stat", s, 0), ("bt", 0), ("bt", 1)])
                    P.op("act", lambda e: e.activation(out=stat[s][:, 1:2], in_=stat[s][:, 0:1], func=AF.Ln,
                                                       scale=1.0 / D, bias=cst[:, 1:2]),
                         r=[("stat", s, 0), "cst"], w=[("stat", s, 1)])
                    P.op("act", lambda e: e.activation(out=stat[s][:, 2:3], in_=stat[s][:, 1:2], func=AF.Exp,
                                                       scale=-0.5, bias=cst[:, 2:3]),
                         r=[("stat", s, 1), "cst"], w=[("stat", s, 2)])

                def S1(tt=tt, st_=st_):
                    s = st_["s"]
                    P.op("dve", lambda e: e.tensor_scalar(out=hbf[s][:], in0=xt[s][:], scalar1=stat[s][:, 2:3],
                                                          scalar2=None, op0=ALU.mult),
                         r=[("xt", s), ("stat", s, 2)], w=["hbf"])
                    b = bank()
                    st_["b"] = b
                    pv = ps[b][:].bitcast(BF16)

                    def f_tr(e):
                        for dc in range(8):
                            i = e.transpose(pv[:, dc * 128:(dc + 1) * 128], hbf[s][:, dc * 128:(dc + 1) * 128], ident[:])
                        return i
                    P.op("pe", f_tr, r=["hbf", "ident"], x=[("ps", b)])

                def S2(tt=tt, st_=st_):
                    b = st_["b"]
                    pv = ps[b][:].bitcast(BF16)
                    P.op("dve", lambda e: e.tensor_tensor(out=xT[:, :, tt * 128:(tt + 1) * 128],
                                                          in0=pv.rearrange("p (c t) -> p c t", c=8),
                                                          in1=normw[:, :].unsqueeze(2).broadcast_to([128, 8, 128]), op=ALU.mult),
                         r=["normw"], w=[("xT", tt)], x=[("ps", b)])
                units.append([S0, S1, S2])
            pipeline(units, 3)

            ck(1)
            def proj_fm(s, c, tq, lo=0, hi=8):
                b = bank(lo, hi)

                def f(e, s=s, c=c, tq=tq, b=b):
                    for dc in range(8):
                        i = e.matmul(ps[b][:], lhsT=wsl[s][:, dc, c * 128:(c + 1) * 128],
                                     rhs=xT[:, dc, tq * 512:(tq + 1) * 512], start=(dc == 0), stop=(dc == 7))
                    return i
                P.op("pe", f, r=[("ws", s)] + [("xT", tq * 4 + i) for i in range(4)], x=[("ps", b)])
                return b

            def qk_unit(s, c, tq, wcol, fin):
                st_ = {}

                def S0():
                    b = proj_fm(s, c, tq, 0, 4)
                    st_["b"] = b
                    sq_ = bt()
                    st_["sq"] = sq_
                    P.op("act", lambda e: e.activation(out=BT[sq_][:], in_=ps[b][:], func=AF.Square), w=[("bt", sq_)], x=[("ps", b)])

                def S1():
                    sq_ = st_["sq"]
                    b2 = bank(4, 6)
                    P.op("pe", lambda e: e.matmul(ps[b2][:], lhsT=bones[:], rhs=BT[sq_][:], start=True, stop=True),
                         r=[("bt", sq_), "bones"], x=[("ps", b2)])
                    ln_ = ft()
                    P.op("act", lambda e: e.activation(out=FT[ln_][:], in_=ps[b2][:], func=AF.Ln, bias=cst[:, 1:2]),
                         r=["cst"], w=[("ft", ln_)], x=[("ps", b2)])
                    rs_ = ft()
                    st_["rs"] = rs_
                    P.op("act", lambda e: e.activation(out=FT[rs_][:], in_=FT[ln_][:], func=AF.Exp, scale=-0.5, bias=cst[:, 2:3]),
                         r=[("ft", ln_), "cst"], w=[("ft", rs_)])

                def S2():
                    b, rs_ = st_["b"], st_["rs"]
                    h_ = ft()
                    P.op("dve", lambda e: e.scalar_tensor_tensor(out=FT[h_][:], in0=ps[b][:], scalar=qkw[:, wcol:wcol + 1],
                                                               in1=FT[rs_][:], op0=ALU.mult, op1=ALU.mult),
                         r=[("ft", rs_), "qkw"], w=[("ft", h_)], x=[("ps", b)])
                    fin(h_)
                return [S0, S1, S2]

            def proj_v(col0):
                s = wload(col0)
                for tt in range(16):
                    b = bank()

                    def f(e, s=s, tt=tt, b=b):
                        for dc in range(8):
                            i = e.matmul(ps[b][:], lhsT=xT[:, dc, tt * 128:(tt + 1) * 128], rhs=wsl[s][:, dc, :],
                                         start=(dc == 0), stop=(dc == 7))
                        return i
                    P.op("pe", f, r=[("ws", s), ("xT", tt)], x=[("ps", b)])
                    P.op("act", lambda e, tt=tt, b=b: e.activation(out=VV[:, tt, :], in_=ps[b][:], func=AF.Copy),
                         w=[("V", tt)], x=[("ps", b)])

            def proj_plain(col0, dst, mk, scale):
                s = wload(col0)
                for tq in range(4):
                    for c in range(4):
                        b = proj_fm(s, c, tq)
                        P.op("dve", lambda e, c=c, tq=tq, b=b: e.tensor_scalar(out=dst[:, c, tq * 512:(tq + 1) * 512], in0=ps[b][:],
                                                                              scalar1=scale, scalar2=None, op0=ALU.mult),
                             w=mk(c, tq), x=[("ps", b)])

            def dump(nm, src_ap, keys):
                if debug:
                    P.dma("sp", "dbg", lambda e: e.dma_start(out=dbg[nm], in_=src_ap), r=keys)

            qakeys = [("qa", c, q) for c in range(4) for q in range(8)]
            qbkeys = [("qb", c, q) for c in range(4) for q in range(4)]

            s = wload(512)
            units = []
            for tq in range(4):
                for c in range(4):
                    def fin(h_, c=c, tq=tq):
                        P.op("dve", lambda e: e.tensor_copy(out=kk[:, c, tq * 512:(tq + 1) * 512], in_=FT[h_][:]),
                             r=[("ft", h_)], w=[("k", c, tq)])
                        P.op("dve", lambda e: e.tensor_reduce(
                            out=kmean[:, c, tq * 2:(tq + 1) * 2], in_=FT[h_][:].rearrange("p (a b) -> p a b", a=2), axis=AX.X, op=ALU.add),
                            r=[("ft", h_)], w=[("kmean", c, tq)])
                    units.append(qk_unit(s, c, tq, 1, fin))
            pipeline(units, 3)
            ck(2)
            s = wload(0)
            units = []
            for tq in range(4):
                gviews = [ps[6 + hh][:, 0:128].rearrange("p (i c n) -> p i c n", i=4, c=4) for hh in range(2)]
                for c in range(4):
                    def fin(h_, c=c, tq=tq, gviews=gviews):
                        P.op("dve", lambda e: e.tensor_scalar(out=qa[:, c, tq * 512:(tq + 1) * 512], in0=FT[h_][:], scalar1=0.125, scalar2=None, op0=ALU.mult),
                             r=[("ft", h_)], w=[("qa", c, 2 * tq), ("qa", c, 2 * tq + 1)])

                        def f_gate(e):
                            for i in range(4):
                                for hh in range(2):
                                    ins = e.matmul(gviews[hh][:, i, c, :], lhsT=FT[h_][hh * 64:(hh + 1) * 64, i * 128:(i + 1) * 128],
                                                   rhs=kmean[hh * 64:(hh + 1) * 64, c, :], start=True, stop=True)
                            return ins
                        P.op("pe", f_gate, r=[("ft", h_)] + [("kmean", c, t) for t in range(4)], x=[("ps", 6), ("ps", 7)])
                    units.append(qk_unit(s, c, tq, 0, fin))

                def topk(tq=tq, gviews=gviews):
                    for i in range(4):
                        tt = tq * 4 + i
                        qblk = tt // 2
                        sl = rr("g", 2)

                        def f_gm(e, sl=sl, i=i, qblk=qblk, gviews=gviews):
                            for hh in range(2):
                                ins = e.tensor_tensor(out=gm[sl][:, hh * 4:(hh + 1) * 4, :], in0=gviews[hh][:, i, :, :],
                                                      in1=cmneg[:, qblk:qblk + 1, :].broadcast_to([128, 4, 8]), op=ALU.add)
                            return ins
                        P.op("dve", f_gm, r=["cmneg"], w=[("gm", sl)], x=[("ps", 6), ("ps", 7)])

                        def f_m8(e, sl=sl):
                            for h in range(8):
                                ins = e.max(out=m8[sl][:, h, :], in_=gm[sl][:, h, :])
                            return ins
                        P.op("dve", f_m8, r=[("gm", sl)], w=[("m8", sl)])
                        P.op("dve", lambda e, sl=sl: e.tensor_tensor(out=sel[sl][:], in0=gm[sl][:], in1=m8[sl][:, :, 2:3].broadcast_to([128, 8, 8]), op=ALU.is_ge),
                             r=[("gm", sl), ("m8", sl)], w=[("sel", sl)])
                        P.op("dve", lambda e, sl=sl: e.tensor_scalar(out=sel[sl][:], in0=sel[sl][:], scalar1=-1.0, scalar2=-NEG, op0=ALU.add, op1=ALU.mult),
                             r=[("sel", sl)], w=[("sel", sl)])
                        P.op("dve", lambda e, sl=sl, qblk=qblk: e.tensor_tensor(out=sel[sl][:], in0=sel[sl][:], in1=cm01[:, qblk:qblk + 1, :].broadcast_to([128, 8, 8]), op=ALU.mult),
                             r=[("sel", sl), "cm01"], w=[("sel", sl)])
                        P.op("dve", lambda e, sl=sl: e.tensor_tensor(out=mv[sl][:], in0=sel[sl][:], in1=b31g[:, :].unsqueeze(2).broadcast_to([128, 8, 8]), op=ALU.add),
                             r=[("sel", sl), "b31g"], w=[("mv", sl)])
                        b = bank(4, 6)
                        pv = ps[b][:].bitcast(BF16)

                        def f_mt(e, sl=sl, pv=pv):
                            for hh in range(2):
                                for c in range(4):
                                    ins = e.transpose(pv[hh * 64:hh * 64 + 8, c * 128:(c + 1) * 128], mv[sl][:, hh * 4 + c, :], ident[:])
                            return ins
                        P.op("pe", f_mt, r=[("mv", sl), "ident"], x=[("ps", b)])

                        def f_mc(e, tt=tt, pv=pv):
                            for hh in range(2):
                                ins = e.activation(out=maskT[hh * 64:hh * 64 + 8, :, tt * 128:(tt + 1) * 128],
                                                   in_=pv[hh * 64:hh * 64 + 8, 0:512].rearrange("p (c t) -> p c t", c=4), func=AF.Copy)
                            return ins
                        P.op("act", f_mc, w=[("maskT", tt)], x=[("ps", b)])

                units.append([None, None, topk])
            pipeline(units, 3)
            ck(3)
            proj_v(1024)
            dump("qa", qa[:], qakeys)
            dump("ka", kk, kkeys)
            dump("va", VV, vkeys)
            dump("mask", maskT[:], [("maskT", t) for t in range(16)])

            ck(4)
            units = []
            it = 0
            for qblk in range(8):
                for c in range(4):
                    ob_, db_ = (4, 5) if it % 2 == 0 else (6, 7)
                    it += 1
                    for n in range(qblk + 1):
                        own = (n == qblk)
                        st_ = {}

                        def A(c=c, n=n, own=own, qblk=qblk, st_=st_):
                            zb = 2 * rr("mzb", 2)
                            st_["zb"] = zb
                            zp = PSP(zb)

                            def f_qk(e):
                                for hh in range(2):
                                    p0 = hh * 64
                                    zv = zp[:, hh, :].rearrange("p (k q) -> p k q", k=2)
                                    for kt in range(2):
                                        c0 = 128 if (own and kt == 1) else 0
                                        e.matmul(zv[:, kt, c0:256], lhsT=kk[p0:p0 + 64, c, (2 * n + kt) * 128:(2 * n + kt + 1) * 128],
                                                 rhs=qa[p0:p0 + 64, c, qblk * 256 + c0:(qblk + 1) * 256], start=True, stop=False)
                                        ins = e.matmul(zv[:, kt, c0:256], lhsT=esel[p0:p0 + 64, n, :], rhs=maskT[p0:p0 + 64, c, qblk * 256 + c0:(qblk + 1) * 256],
                                                       start=False, stop=True)
                                return ins
                            P.op("pe", f_qk, r=[("k", c, n // 2), ("qa", c, qblk), "esel", ("maskT", 2 * qblk), ("maskT", 2 * qblk + 1)],
                                 x=[("ps", zb), ("ps", zb + 1)])
                            if own or n == qblk - 1:
                                def f_b(e):
                                    for hh in range(2):
                                        h = 2 * c + hh
                                        zv = zp[:, hh, :].rearrange("p (k q) -> p k q", k=2)
                                        if own:
                                            e.tensor_tensor(out=zv[:, 0, 0:128], in0=zv[:, 0, 0:128], in1=biasD[:, h, :], op=ALU.add)
                                            e.tensor_tensor(out=zv[:, 0, 128:256], in0=zv[:, 0, 128:256], in1=biasO[:, h, :], op=ALU.add)
                                            ins = e.tensor_tensor(out=zv[:, 1, 128:256], in0=zv[:, 1, 128:256], in1=biasD[:, h, :], op=ALU.add)
                                        else:
                                            ins = e.tensor_tensor(out=zv[:, 1, 0:128], in0=zv[:, 1, 0:128], in1=biasO[:, h, :], op=ALU.add)
                                    return ins
                                P.op("dve", f_b, r=["biasD", "biasO"], x=[("ps", zb), ("ps", zb + 1)])
                            pp = btp()
                            st_["pp"] = pp
                            P.op("act", lambda e: e.activation(out=BTP(pp), in_=zp, func=AF.Exp, bias=cst[:, 2:3]),
                                 r=["cst"], w=[("bt", pp), ("bt", pp + 1)], x=[("ps", zb), ("ps", zb + 1)])

                        def B(c=c, n=n, own=own, qblk=qblk, st_=st_, ob_=ob_, db_=db_):
                            pp = st_["pp"]

                            def f_av(e):
                                for hh in range(2):
                                    p0 = hh * 64
                                    h = 2 * c + hh
                                    pvv = BTP(pp)[:, hh, :].rearrange("p (k q) -> p k q", k=2)
                                    for kt in range(2):
                                        c0 = 128 if (own and kt == 1) else 0
                                        first = (n == 0 and kt == 0)
                                        last = (own and kt == 1)
                                        e.matmul(ps[ob_][p0:p0 + 64, c0:256], lhsT=VV[:, 2 * n + kt, h * 64:(h + 1) * 64], rhs=pvv[:, kt, c0:256],
                                                 start=first, stop=last, skip_group_check=True)
                                        ins = e.matmul(ps[db_][p0:p0 + 64, c0:256], lhsT=ones[:, 0:64], rhs=pvv[:, kt, c0:256],
                                                       start=first, stop=last, skip_group_check=True)
                                return ins
                            P.op("pe", f_av, r=[("bt", pp), ("bt", pp + 1), ("V", 2 * n), ("V", 2 * n + 1), "ones"], x=[("ps", ob_), ("ps", db_)])
                            if own:
                                rd_ = ft()
                                P.op("dve", lambda e: e.reciprocal(out=FT[rd_][:, 0:256], in_=ps[db_][:, 0:256]), w=[("ft", rd_)], x=[("ps", db_)])
                                P.op("dve", lambda e: e.tensor_tensor(out=qa[:, c, qblk * 256:(qblk + 1) * 256], in0=ps[ob_][:, 0:256],
                                                                      in1=FT[rd_][:, 0:256], op=ALU.mult),
                                     r=[("ft", rd_)], w=[("qa", c, qblk)], x=[("ps", ob_)])
                        units.append([A, B])
            units = [[u_[0], None, u_[1]] for u_ in units]
            pipeline(units, 3)
            wstate["mask_free"] = True
            dump("oa", qa[:], qakeys)

            ck(5)
            proj_plain(2048, qb_, lambda c, tq: [("qb", c, tq)], 0.125)
            proj_plain(2560, kk, lambda c, tq: [("k", c, tq)], 1.0)
            proj_v(3072)
            dump("qb", qb_[:], qbkeys)
            dump("kb", kk, kkeys)
            dump("vb", VV, vkeys)

            ck(6)
            units = []
            for qc in range(4):
                for c in range(4):
                    ob_ = 6
                    nkb = 4 * qc + 4
                    for u in range(nkb):
                        kbk = nkb - 1 - u
                        j = kbk - 4 * qc
                        c0 = j * 128 if j >= 0 else 0
                        st_ = {}

                        def A(c=c, qc=qc, u=u, kbk=kbk, j=j, c0=c0, st_=st_):
                            zb = 2 * rr("szb", 2)
                            zp = PSP(zb)

                            def f_z(e):
                                for hh in range(2):
                                    p0 = hh * 64
                                    ins = e.matmul(zp[:, hh, c0:512], lhsT=kk[p0:p0 + 64, c, kbk * 128:(kbk + 1) * 128],
                                                   rhs=qb_[p0:p0 + 64, c, qc * 512 + c0:(qc + 1) * 512], start=True, stop=(j < 0))
                                    if j >= 0:
                                        ins = e.matmul(zp[:, hh, c0:c0 + 128], lhsT=ident[:], rhs=negm[:], start=False, stop=True)
                                return ins
                            P.op("pe", f_z, r=[("k", c, kbk // 4), ("qb", c, qc), "ident", "negm"], x=[("ps", zb), ("ps", zb + 1)])
                            e_ = 2 * rr('sbE', 3)
                            st_["e"] = e_
                            ekeys = [("ft", e_), ("ft", e_ + 1)]
                            P.op("act", lambda e: e.activation(out=FTP(e_)[:, :, c0:512], in_=zp[:, :, c0:512], func=AF.Exp, bias=cst[:, 2:3]),
                                 r=["cst"], w=ekeys, x=[("ps", zb), ("ps", zb + 1)])
                            L_ = btp()
                            st_["L"] = L_
                            P.op("act", lambda e: e.activation(out=BTP(L_)[:, :, c0:512], in_=FTP(e_)[:, :, c0:512], func=AF.Ln, bias=cst[:, 0:1]),
                                 r=ekeys + ["cst"], w=[("bt", L_), ("bt", L_ + 1)])

                        def B(c=c, qc=qc, u=u, kbk=kbk, c0=c0, st_=st_, nkb=nkb):
                            e_, L_ = st_["e"], st_["L"]
                            cp = PSP(4)
                            cprev, ccur = (u + 1) % 2, u % 2
                            if u == 0:
                                for i_ in range(2):
                                    P.op("dve", lambda e, i_=i_: e.memset(carry[i_][0:33, :], 0.0), w=[("carry", i_)])

                            def f_cs(e):
                                for hh in range(2):
                                    if u < nkb - 1:
                                        e.matmul(ps[7][:, c0:512], lhsT=csel[:, hh, :], rhs=BTP(L_)[:, hh, c0:512],
                                                 start=(u == 0 and hh == 0), stop=False, skip_group_check=True)
                                    ins = e.matmul(cp[:, hh, c0:512], lhsT=tri[:], rhs=BTP(L_)[:, hh, c0:512], start=True, stop=(u == 0))
                                    if u > 0:
                                        ins = e.matmul(cp[:, hh, c0:512], lhsT=rsel[:, hh, :], rhs=carry[cprev][:, c0:512], start=False, stop=True)
                                return ins
                            P.op("pe", f_cs, r=[("bt", L_), ("bt", L_ + 1), ("carry", cprev), "tri", "csel", "rsel"], x=[("ps", 4), ("ps", 5), ("ps", 7)])
                            if u < nkb - 1:
                                P.op("dve", lambda e: e.tensor_copy(out=carry[ccur][0:33, c0:512], in_=ps[7][0:33, c0:512]),
                                     w=[("carry", ccur)], x=[("ps", 7)])
                            g_ = 6 + 2 * rr('sbG', 2)
                            gkeys = [("ft", g_), ("ft", g_ + 1)]
                            P.op("act", lambda e: e.activation(out=FTP(g_)[:, :, c0:512], in_=cp[:, :, c0:512], func=AF.Exp, scale=-1.0, bias=cst[:, 2:3]),
                                 r=["cst"], w=gkeys, x=[("ps", 4), ("ps", 5)])
                            w_ = btp()
                            st_["w"] = w_
                            P.op("dve", lambda e: e.tensor_tensor(out=BTP(w_)[:, :, c0:512], in0=FTP(e_)[:, :, c0:512], in1=FTP(g_)[:, :, c0:512], op=ALU.mult),
                                 r=[("ft", e_), ("ft", e_ + 1)] + gkeys, w=[("bt", w_), ("bt", w_ + 1)])

                        def C(c=c, qc=qc, u=u, kbk=kbk, c0=c0, st_=st_, nkb=nkb, ob_=ob_):
                            w_ = st_["w"]

                            def f_av(e):
                                for hh in range(2):
                                    p0 = hh * 64
                                    h = 2 * c + hh
                                    ins = e.matmul(ps[ob_][p0:p0 + 64, c0:512], lhsT=VV[:, kbk, h * 64:(h + 1) * 64], rhs=BTP(w_)[:, hh, c0:512],
                                                   start=(u == 0), stop=(u == nkb - 1), skip_group_check=True)
                                return ins
                            P.op("pe", f_av, r=[("bt", w_), ("bt", w_ + 1), ("V", kbk)], x=[("ps", ob_)])
                            if u == nkb - 1:
                                P.op("dve", lambda e: e.tensor_copy(out=qb_[:, c, qc * 512:(qc + 1) * 512], in_=ps[ob_][:]),
                                     w=[("qb", c, qc)], x=[("ps", ob_)])
                        units.append([A, B, C])
            pipeline(units, 3)
            dump("ob", qb_[:], qbkeys)

            ck(7)
            P.dma("pool", "w6", lambda e: e.dma_start(out=wupa, in_=wua_d.rearrange("(c p) n -> p c n", p=128)), w=["wup"])
            P.dma("pool", "w6", lambda e: e.dma_start(out=wupb, in_=wub_d.rearrange("(c p) n -> p c n", p=128)), w=["wup"])
            P.dma("pool", "w6", lambda e: e.dma_start(out=wout, in_=wo_d.rearrange("(c p) n -> p c n", p=128)), w=["wout"])
            w6keys = ["wup", "wout"]
            units = []
            for tq in range(4):
                xkeys = [("xT", tq * 4 + i) for i in range(4)]
                for br, (zcol, Osrc) in enumerate(((1536, qa), (3584, qb_))):
                    slot_ = {}
                    for c in range(4):
                        st_ = {}

                        def S0(br=br, zcol=zcol, c=c, tq=tq, st_=st_, slot_=slot_):
                            if c == 0:
                                slot_["s"] = wload(zcol)
                            st_["b"] = proj_fm(slot_["s"], c, tq)

                        def S1(br=br, Osrc=Osrc, c=c, tq=tq, st_=st_):
                            b = st_["b"]
                            sg_ = 4 + rr('ft6', 4)
                            P.op("act", lambda e: e.activation(out=FT[sg_][:], in_=ps[b][:], func=AF.Sigmoid, bias=cst[:, 2:3]),
                                 r=["cst"], w=[("ft", sg_)], x=[("ps", b)])
                            t_ = 4 + rr('ft6', 4)
                            P.op("dve", lambda e: e.tensor_tensor(out=FT[t_][:], in0=ps[b][:], in1=FT[sg_][:], op=ALU.mult),
                                 r=[("ft", sg_)], w=[("ft", t_)], x=[("ps", b)])
                            okeys = [("qa", c, 2 * tq), ("qa", c, 2 * tq + 1)] if br == 0 else [("qb", c, tq)]
                            P.op("dve", lambda e: e.tensor_tensor(out=Gt[br][:, c, :], in0=FT[t_][:],
                                                                  in1=Osrc[:, c, tq * 512:(tq + 1) * 512], op=ALU.mult),
                                 r=[("ft", t_)] + okeys, w=[("bt", br * 4 + c)])
                        units.append([S0, S1])
                units.append([None, None])
                for half in range(2):
                    slot_ = {}
                    for f4 in range(4):
                        f = half * 4 + f4
                        st_ = {}

                        def S0(half=half, f4=f4, f=f, tq=tq, st_=st_, slot_=slot_, xkeys=xkeys):
                            if f4 == 0:
                                slot_["a"] = wload(4096 + half * 512)
                                slot_["b"] = wload(5120 + half * 512)
                            bya, byb, bga, bgb = bank(), bank(), bank(), bank()
                            st_["banks"] = (bya, byb, bga, bgb)
                            for br, (bg, sgw) in enumerate(((bga, slot_["a"]), (bgb, slot_["b"]))):
                                def f_g(e, bg=bg, sgw=sgw):
                                    for dc in range(8):
                                        ins = e.matmul(ps[bg][:], lhsT=wsl[sgw][:, dc, f4 * 128:(f4 + 1) * 128],
                                                       rhs=xT[:, dc, tq * 512:(tq + 1) * 512], start=(dc == 0), stop=(dc == 7))
                                    return ins
                                P.op("pe", f_g, r=[("ws", sgw)] + xkeys, x=[("ps", bg)])
                            for br, (by, wup_) in enumerate(((bya, wupa), (byb, wupb))):
                                def f_up(e, by=by, wup_=wup_, br=br):
                                    for kc in range(4):
                                        ins = e.matmul(ps[by][:], lhsT=wup_[:, kc, f * 128:(f + 1) * 128], rhs=Gt[br][:, kc, :], start=(kc == 0), stop=(kc == 3))
                                    return ins
                                P.op("pe", f_up, r=w6keys + [("bt", br * 4 + kc) for kc in range(4)], x=[("ps", by)])

                        def S1(f=f, st_=st_):
                            bya, byb, bga, bgb = st_["banks"]
                            tms = []
                            for br, (bg, by) in enumerate(((bga, bya), (bgb, byb))):
                                sg_ = 4 + rr('ft6', 4)
                                P.op("act", lambda e, sg_=sg_, bg=bg, br=br: e.activation(out=FT[sg_][:], in_=ps[bg][:], func=AF.Sigmoid, bias=mgb[:, br * 8 + f: br * 8 + f + 1]),
                                     r=["mgb"], w=[("ft", sg_)], x=[("ps", bg)])
                                tm_ = 4 + rr('ft6', 4)
                                tms.append(tm_)
                                P.op("dve", lambda e, sg_=sg_, tm_=tm_, by=by: e.tensor_tensor(out=FT[tm_][:], in0=ps[by][:], in1=FT[sg_][:], op=ALU.mult),
                                     r=[("ft", sg_)], w=[("ft", tm_)], x=[("ps", by)])
                            P.op("dve", lambda e, ta=tms[0], tb=tms[1]: e.tensor_tensor(out=yT[:, f, :], in0=FT[ta][:], in1=FT[tb][:], op=ALU.add),
                                 r=[("ft", tms[0]), ("ft", tms[1])], w=[("ft", f // 2)])
                        units.append([S0, S1])
                units.append([None, None])
                for i in range(4):
                    tt = tq * 4 + i
                    xs_ = {}
                    for half in range(2):
                        st_ = {}

                        def S0(i=i, tt=tt, half=half, st_=st_, xs_=xs_):
                            if half == 0:
                                s = rr("x", 2)
                                xs_["s"] = s
                                src = x_d[t0 + tt * 128: t0 + (tt + 1) * 128, :]
                                P.dma("sp", f"x{s}", lambda e: e.dma_start(out=xt[s][:], in_=src), w=[("xt", s)])
                            b = bank()
                            st_["b"] = b

                            def f_o(e):
                                for fc in range(8):
                                    ins = e.matmul(ps[b][:], lhsT=yT[:, fc, i * 128:(i + 1) * 128], rhs=wout[:, fc, half * 512:(half + 1) * 512],
                                                   start=(fc == 0), stop=(fc == 7))
                                return ins
                            P.op("pe", f_o, r=w6keys + [("ft", fc) for fc in range(4)], x=[("ps", b)])

                        def S1(tt=tt, half=half, st_=st_, xs_=xs_):
                            b, s = st_["b"], xs_["s"]
                            P.op("dve", lambda e: e.tensor_tensor(out=xt[s][:, half * 512:(half + 1) * 512], in0=ps[b][:],
                                                                  in1=xt[s][:, half * 512:(half + 1) * 512], op=ALU.add),
                                 r=[("xt", s)], w=[("xt", s)], x=[("ps", b)])
                            if half == 1:
                                dst = out_d[t0 + tt * 128: t0 + (tt + 1) * 128, :]
                                P.dma("sp", f"st{s}", lambda e: e.dma_start(out=dst, in_=xt[s][:]), r=[("xt", s)])
                        units.append([S0, S1])
            pipeline(units, 2)

        try:
            for seq in range(nseq):
                seq_body(seq)
        except _Stop:
            pass
        print('ops', {k: len(v) for k, v in P.ops.items()})
        block = es.enter_context(nc.Block())
        for en in COMPUTE:
            k = 0
            for op in P.ops[en]:
                if op.signal and op.dma is None:
                    k += 1
                    op.sig_idx = k

        def emit(en, eh):
            for op in P.ops[en]:
                for key, val in op.waits.items():
                    if isinstance(key, tuple):
                        eh.wait_ge(sem("d_" + key[1]), val)
                    else:
                        eh.wait_ge(sem("c_" + key), P.ops[key][val - 1].sig_idx)
                ins = op.fn(eh)
                if op.dma is not None:
                    ins.then_inc(sem("d_" + op.dma[0]), 16)
                elif op.signal:
                    ins.then_inc(sem("c_" + en), 1)
            if en == "sp":
                for name, c in P.dma_count.items():
                    if name.startswith("st") or name == "dbg":
                        eh.wait_ge(sem("d_" + name), c)

        for name in P.dma_count:
            sem("d_" + name)

        @block.tensor
        def _(e):
            emit("pe", e)

        @block.scalar
        def _(e):
            emit("act", e)

        @block.vector
        def _(e):
            emit("dve", e)

        @block.gpsimd
        def _(e):
            emit("pool", e)

        @block.sync
        def _(e):
            emit("sp", e)
    return nc


_CONST = None


def _consts():
    global _CONST
    if _CONST is None:
        bf = ml_dtypes.bfloat16
        i = np.arange(128)
        c = {}
        c["ident"] = np.eye(128, dtype=np.float32).astype(bf)
        c["bones"] = ((i[:, None] // 64 == i[None, :] // 64).astype(np.float32) / 64.0).astype(bf)
        c["tri"] = (i[:, None] >= i[None, :]).astype(np.float32).astype(bf)
        c["ones"] = np.ones((128, 128), np.float32).astype(bf)
        c["smask"] = (i[:, None] < i[None, :]).astype(np.float32)
        es = np.zeros((128, 8, 128), np.float32)
        for n in range(8):
            es[n, n, :] = 1.0
            es[64 + n, n, :] = 1.0
        c["esel"] = es.astype(bf)
        n = np.arange(8)
        cm = (n[None, :] < n[:, None]).astype(np.float32)
        c["cm01"] = np.ascontiguousarray(np.broadcast_to(cm[None], (128, 8, 8))).astype(np.float32)
        c["cmneg"] = np.ascontiguousarray(np.broadcast_to(((1.0 - cm) * -1e30)[None], (128, 8, 8))).astype(np.float32)
        cs_ = np.zeros((128, 2, 128), np.float32)
        rs_ = np.zeros((128, 2, 128), np.float32)
        for hh in range(2):
            cs_[:, hh, 32 * hh] = 1.0
            rs_[32 * hh, hh, :] = 1.0
        c["csel"] = cs_.astype(bf)
        c["rsel"] = rs_.astype(bf)
        c["negm"] = np.where(i[:, None] >= i[None, :], NEG, 0.0).astype(np.float32).astype(bf)
        cst = np.zeros((128, 4), np.float32)
        cst[:, 0] = 1.0
        cst[:, 1] = 1e-6
        c["cst"] = cst
        bk = _bucket_table(512)
        d = i[None, :] - i[:, None]
        c["idxD"] = bk[np.maximum(d, 0)]
        c["mskD"] = d >= 0
        c["idxO"] = bk[128 + d]
        _CONST = c
    return _CONST


def _prep(inputs):
    c = _consts()
    f32 = np.float32
    rb = np.asarray(inputs["rel_bias"], f32)
    m = {
        "w_in": np.ascontiguousarray(np.asarray(inputs["w_in"], f32)[0]),
        "w_up_a": np.ascontiguousarray(np.asarray(inputs["w_up_moba"], f32)[0]),
        "w_up_b": np.ascontiguousarray(np.asarray(inputs["w_up_sb"], f32)[0]),
        "w_out": np.ascontiguousarray(np.asarray(inputs["w_out"], f32)[0]),
        "normw": np.ascontiguousarray(np.asarray(inputs["norm_w"], f32)[0].reshape(8, 128).T),
        "qkw": np.ascontiguousarray(np.stack([np.tile(np.asarray(inputs["q_norm_w"], f32)[0], 2),
                                              np.tile(np.asarray(inputs["k_norm_w"], f32)[0], 2)], axis=1)),
        "mgb": np.ascontiguousarray(np.asarray(inputs["merge_gate_b"], f32)[0].reshape(16, 128).T),
        "b31": np.ascontiguousarray(np.broadcast_to(rb[None, :, 31], (128, 8))),
        "b31g": np.ascontiguousarray(np.broadcast_to(rb[None, [0, 2, 4, 6, 1, 3, 5, 7], 31], (128, 8))),
    }
    bd = rb[:, c["idxD"]]
    bd = np.where(c["mskD"][None], bd, f32(NEG))
    m["biasD"] = np.ascontiguousarray(bd.transpose(1, 0, 2)).astype(f32)
    m["biasO"] = np.ascontiguousarray(rb[:, c["idxO"]].transpose(1, 0, 2)).astype(f32)
    for k in ("cst", "ident", "bones", "tri", "ones", "smask", "esel", "cmneg", "cm01", "csel", "rsel", "negm"):
        m[k] = c[k]
    return m


_NC = {}


def kernel(**inputs):
    x = np.asarray(inputs["x"], np.float32)
    B = x.shape[0]
    ncores = 8
    nseq = B // ncores
    shared = _prep(inputs)
    if nseq not in _NC:
        _NC[nseq] = build(nseq)
    nc = _NC[nseq]
    in_maps = []
    for i in range(ncores):
        m = dict(shared)
        m["x"] = np.ascontiguousarray(x[i * nseq:(i + 1) * nseq].reshape(nseq * S, D))
        in_maps.append(m)
    res = run_bass_kernel_spmd(nc, in_maps, core_ids=list(range(ncores)))
    out = np.stack([np.asarray(r["out"], np.float32).reshape(nseq, S, D) for r in res.results], axis=0)
    return out.reshape(B, S, D)
```

```python
import numpy as np
import ml_dtypes
import concourse.bass as bass
import concourse.mybir as mybir
from concourse.bass_utils import run_bass_kernel_spmd

F32 = mybir.dt.float32
BF16 = mybir.dt.bfloat16
AF = mybir.ActivationFunctionType
ALU = mybir.AluOpType
AX = mybir.AxisListType

S = 2048
D = 1024
NEG = -30000.0
COMPUTE = ("pe", "act", "dve", "pool")


class Op:
    __slots__ = ("eng", "fn", "seq", "waits", "signal", "dma", "sig_idx")


class Prog:
    def __init__(self):
        self.ops = {e: [] for e in COMPUTE + ("sp",)}
        self.last_w = {}
        self.readers = {}
        self.alias = {}
        self.known = {e: {} for e in COMPUTE + ("sp",)}
        self.dma_count = {}

    def _dep(self, op, prod):
        if prod is op:
            return
        if prod.dma is not None:
            key, val = ("dma", prod.dma[0]), self.dma_count[prod.dma[0]]
        else:
            if prod.eng == op.eng and op.dma is None and op.eng == "pe":
                return
            key, val = prod.eng, prod.seq
            prod.signal = True
        if self.known[op.eng].get(key, 0) >= val:
            return
        if op.waits.get(key, 0) < val:
            op.waits[key] = val

    def _add(self, eng, fn, r, w, x, dma):
        op = Op()
        op.eng, op.fn, op.waits, op.signal, op.dma, op.sig_idx = eng, fn, {}, False, dma, 0
        lst = self.ops[eng]
        op.seq = len(lst) + 1
        wkeys = []
        for k in tuple(w) + tuple(x):
            wkeys.append(k)
            for a in self.alias.get(k, ()):
                wkeys.append(a)
        for k in r:
            p = self.last_w.get(k)
            if p is not None:
                if p.eng == eng and p.dma is None and dma is None and eng != "pe":
                    p.signal = True
                    if self.known[eng].get(eng, 0) < p.seq and op.waits.get(eng, 0) < p.seq:
                        op.waits[eng] = p.seq
                else:
                    self._dep(op, p)
        for k in wkeys:
            p = self.last_w.get(k)
            if p is not None:
                self._dep(op, p)
            for q in self.readers.get(k, ()):
                self._dep(op, q)
        for k in x:
            pass
        for k in r:
            self.readers.setdefault(k, []).append(op)
        for k in wkeys:
            self.last_w[k] = op
            self.readers[k] = []
        for key, val in op.waits.items():
            if self.known[eng].get(key, 0) < val:
                self.known[eng][key] = val
        lst.append(op)
        return op

    def op(self, eng, fn, r=(), w=(), x=()):
        return self._add(eng, fn, r, w, x, None)

    def dma(self, eng, semname, fn, r=(), w=()):
        c = self.dma_count.get(semname, 0) + 16
        self.dma_count.setdefault(semname, 0)
        op = self._add(eng, fn, r, w, (), (semname, c))
        self.dma_count[semname] = c
        return op


def _bucket_table(maxd):
    n = np.arange(maxd, dtype=np.int64)
    nf = np.maximum(n, 1).astype(np.float32)
    large = 16 + (np.log(nf / np.float32(16)) / np.float32(np.log(128 / 16)) * np.float32(16)).astype(np.int32)
    large = np.minimum(large, 31)
    return np.where(n < 16, n, large).astype(np.int64)


class _Stop(Exception):
    pass


def build(nseq, debug=False, stop=99):
    nc = bass.Bass("TRN2", target_bir_lowering=False)
    P = Prog()

    def din(name, shape, dt=F32):
        return nc.dram_tensor(name, list(shape), dt, kind="ExternalInput").ap()

    x_d = din("x", [nseq * S, D])
    win_d = din("w_in", [D, 6144])
    wua_d = din("w_up_a", [512, D])
    wub_d = din("w_up_b", [512, D])
    wo_d = din("w_out", [D, D])
    normw_d = din("normw", [128, 8])
    qkw_d = din("qkw", [128, 2])
    mgb_d = din("mgb", [128, 16])
    biasD_d = din("biasD", [128, 8, 128])
    biasO_d = din("biasO", [128, 8, 128])
    b31_d = din("b31", [128, 8])
    b31g_d = din("b31g", [128, 8])
    cst_d = din("cst", [128, 4])
    ident_d = din("ident", [128, 128], BF16)
    bones_d = din("bones", [128, 128], BF16)
    tri_d = din("tri", [128, 128], BF16)
    ones_d = din("ones", [128, 128], BF16)
    smask_d = din("smask", [128, 128])
    esel_d = din("esel", [128, 8, 128], BF16)
    cmneg_d = din("cmneg", [128, 8, 8])
    cm01_d = din("cm01", [128, 8, 8])
    csel_d = din("csel", [128, 2, 128], BF16)
    rsel_d = din("rsel", [128, 2, 128], BF16)
    negm_d = din("negm", [128, 128], BF16)
    out_d = nc.dram_tensor("out", [nseq * S, D], F32, kind="ExternalOutput").ap()
    dbg = {}
    if debug:
        for nm in ("qa", "ka", "qb", "kb", "oa", "ob"):
            dbg[nm] = nc.dram_tensor("dbg_" + nm, [128, 4, S], BF16, kind="ExternalOutput").ap()
        for nm in ("va", "vb"):
            dbg[nm] = nc.dram_tensor("dbg_" + nm, [128, 16, 512], BF16, kind="ExternalOutput").ap()
        dbg["mask"] = nc.dram_tensor("dbg_mask", [128, 4, S], BF16, kind="ExternalOutput").ap()

    from contextlib import ExitStack
    es = ExitStack()
    with es:
        def sb(name, shape, dt=F32):
            return es.enter_context(nc.sbuf_tensor(name, list(shape), dt))

        xT = sb("xT", [128, 8, S], BF16)
        qa = sb("qa", [128, 4, S], BF16)
        qb_ = sb("qb", [128, 4, S], BF16)
        R = sb("R", [128, 16384], BF16)
        kk = R[:, 0:8192].rearrange("p (c t) -> p c t", c=4)
        VV = R[:, 8192:16384].rearrange("p (t f) -> p t f", t=16)
        wupa = R[:, 0:4096].rearrange("p (c n) -> p c n", c=4)
        wupb = R[:, 4096:8192].rearrange("p (c n) -> p c n", c=4)
        wout = R[:, 8192:16384].rearrange("p (c n) -> p c n", c=8)
        NWS = 2
        wsl = [sb(f"wsl{i}", [128, 8, 512], BF16)[:] for i in range(NWS)]
        maskT = sb("maskT", [128, 4, S], BF16)
        _mflat = maskT[:].rearrange("p c t -> p (c t)")
        wsl = wsl + [_mflat[:, i * 4096:(i + 1) * 4096].rearrange("p (c n) -> p c n", c=8) for i in range(2)]
        normw = sb("normw_s", [128, 8])
        qkw = sb("qkw_s", [128, 2])
        mgb = sb("mgb_s", [128, 16])
        biasD = sb("biasD_s", [128, 8, 128])
        biasO = sb("biasO_s", [128, 8, 128])
        b31 = sb("b31_s", [128, 8])
        cst = sb("cst_s", [128, 4])
        ident = sb("ident_s", [128, 128], BF16)
        bones = sb("bones_s", [128, 128], BF16)
        tri = sb("tri_s", [128, 128], BF16)
        ones = sb("ones_s", [128, 128], BF16)
        smask = sb("smask_s", [128, 128])
        esel = sb("esel_s", [128, 8, 128], BF16)
        cmneg = sb("cmneg_s", [128, 8, 8])
        cm01 = sb("cm01_s", [128, 8, 8])
        csel = sb("csel_s", [128, 2, 128], BF16)
        rsel = sb("rsel_s", [128, 2, 128], BF16)
        negm = sb("negm_s", [128, 128], BF16)
        xt = [sb(f"xt{i}", [128, D]) for i in range(2)]
        hbf = [sb("hbf0", [128, D], BF16)] * 2
        stat = [sb(f"stat{i}", [128, 4]) for i in range(2)]
        NFT, NBT = 10, 8
        FTall = sb("ftall", [128, NFT * 512])
        BTall = sb("btall", [128, NBT * 512], BF16)
        FT = [FTall[:, i * 512:(i + 1) * 512] for i in range(NFT)]
        BT = [BTall[:, i * 512:(i + 1) * 512] for i in range(NBT)]
        sqj = BTall[:, 0:1024]
        Gt = [BTall[:, br * 2048:(br + 1) * 2048].rearrange("p (c t) -> p c t", c=4) for br in range(2)]
        yT = FTall[:, 0:2048].bitcast(BF16).rearrange("p (f t) -> p f t", f=8)
        FTP = lambda i: FTall[:, i * 512:(i + 2) * 512].rearrange("p (a f) -> p a f", a=2)
        BTP = lambda i: BTall[:, i * 512:(i + 2) * 512].rearrange("p (a f) -> p a f", a=2)
        kmean = sb("kmean", [128, 4, 8])
        gm = [sb(f"gm{i}", [128, 8, 8]) for i in range(2)]
        m8 = [sb(f"m8{i}", [128, 8, 8]) for i in range(2)]
        sel = [sb(f"sel{i}", [128, 8, 8]) for i in range(2)]
        mv = [sb(f"mv{i}", [128, 8, 8], BF16) for i in range(2)]
        carry = [sb(f"carry{i}", [128, 512], BF16) for i in range(2)]
        b31g = sb("b31g_s", [128, 8])
        psall = es.enter_context(nc.psum_tensor("psall", [128, 4096], F32))
        ps = [psall[:, i * 512:(i + 1) * 512] for i in range(8)]
        PSP = lambda b: psall[:, b * 512:(b + 2) * 512].rearrange("p (a f) -> p a f", a=2)

        sems = {}

        def sem(name):
            if name not in sems:
                sems[name] = es.enter_context(nc.semaphore(name))
            return sems[name]

        for e in COMPUTE:
            sem("c_" + e)

        kkeys = [("k", c, t) for c in range(4) for t in range(4)]
        vkeys = [("V", t) for t in range(16)]
        mkeys = [("maskT", t) for t in range(16)]
        P.alias[("ws", 2)] = mkeys
        P.alias[("ws", 3)] = mkeys
        for k in mkeys:
            P.alias[k] = [("ws", 2), ("ws", 3)]
        P.alias["wup"] = kkeys
        P.alias["wout"] = vkeys
        for k in kkeys:
            P.alias[k] = ["wup"]
        for k in vkeys:
            P.alias[k] = ["wout"]

        def cload(dst, src, key):
            P.dma("sp", "const", lambda e, dst=dst, src=src: e.dma_start(out=dst, in_=src), w=[key])

        for dst, src, key in [
            (normw[:], normw_d, "normw"), (qkw[:], qkw_d, "qkw"), (mgb[:], mgb_d, "mgb"),
            (biasD[:], biasD_d, "biasD"), (biasO[:], biasO_d, "biasO"), (b31[:], b31_d, "b31"),
            (cst[:], cst_d, "cst"), (b31g[:], b31g_d, "b31g"), (ident[:], ident_d, "ident"), (bones[:], bones_d, "bones"),
            (tri[:], tri_d, "tri"), (ones[:], ones_d, "ones"), (smask[:], smask_d, "smask"),
            (esel[:], esel_d, "esel"), (cmneg[:], cmneg_d, "cmneg"), (cm01[:], cm01_d, "cm01"),
            (csel[:], csel_d, "csel"), (rsel[:], rsel_d, "rsel"), (negm[:], negm_d, "negm"),
        ]:
            cload(dst, src, key)
        for h in range(8):
            P.op("dve", lambda e, h=h: e.tensor_scalar(out=biasD[:, h, :], in0=biasD[:, h, :], scalar1=b31[:, h:h + 1],
                                                      scalar2=None, op0=ALU.subtract), r=["b31", "biasD"], w=["biasD"])
            P.op("dve", lambda e, h=h: e.tensor_scalar(out=biasO[:, h, :], in0=biasO[:, h, :], scalar1=b31[:, h:h + 1],
                                                      scalar2=None, op0=ALU.subtract), r=["b31", "biasO"], w=["biasO"])

        P.op("pool", lambda e: e.memset(maskT[:], 0.0), w=[("maskT", t) for t in range(16)])
        for i_ in range(2):
            P.op("pool", lambda e, i_=i_: e.memset(carry[i_][:], 0.0), w=[("carry", i_)])
        cnt = {}

        def rr(name, n):
            v = cnt.get(name, 0)
            cnt[name] = v + 1
            return v % n

        def ft():
            return rr("ft", NFT)

        def bt():
            return rr("bt", NBT)

        def ftp():
            return 2 * rr("ftp", NFT // 2)

        def btp():
            return 2 * rr("btp", NBT // 2)

        def pipeline(units, nst):
            n = len(units)
            for t in range(n + nst - 1):
                for st in range(nst):
                    u = t - st
                    if 0 <= u < n and units[u][st] is not None:
                        units[u][st]()

        wseq = []
        for _ in range(nseq):
            wseq += [512, 0, 1024, 2048, 2560, 3072]
            for _t in range(4):
                wseq += [1536, 3584, 4096, 5120, 4608, 5632]
        wstate = {"issued": 0, "use": 0, "mask_free": False}

        def wslot(k):
            j = k % 30
            return [0, 1, 0, 1, 2, 3][j] if j < 6 else (j - 6) % 4

        def wload(col0):
            i = wstate["use"]
            wstate["use"] = i + 1
            assert wseq[i] == col0, (i, wseq[i], col0)
            while wstate["issued"] <= min(i + 3, len(wseq) - 1):
                k = wstate["issued"]
                sl_ = wslot(k)
                if sl_ >= 2 and not wstate["mask_free"]:
                    assert k > i, (k, i)
                    break
                prev = [p for p in range(k) if wslot(p) == sl_]
                if prev and prev[-1] > i - 2 and k > i:
                    break
                src = win_d[:, wseq[k]:wseq[k] + 512].rearrange("(c p) n -> p c n", p=128)
                P.dma("pool", f"w{sl_}", lambda e, sl_=sl_, src=src: e.dma_start(out=wsl[sl_], in_=src), w=[("ws", sl_)])
                wstate["issued"] = k + 1
            return wslot(i)

        def bank(lo=0, hi=8, name=None):
            return lo + rr(name or ("bank%d_%d" % (lo, hi)), hi - lo)

        def seq_body(seq):
            t0 = seq * S

            def ck(k):
                if stop <= k:
                    raise _Stop()
            ck(0)
            wstate["mask_free"] = False
            units = []
            for tt in range(16):
                st_ = {}

                def S0(tt=tt, st_=st_):
                    s = rr("x", 2)
                    st_["s"] = s
                    src = x_d[t0 + tt * 128: t0 + (tt + 1) * 128, :]
                    P.dma("sp", f"x{s}", lambda e: e.dma_start(out=xt[s][:], in_=src), w=[("xt", s)])
                    P.op("act", lambda e: e.activation(out=sqj, in_=xt[s][:], func=AF.Square, accum_out=stat[s][:, 0:1]),
                         r=[("xt", s)], w=[("stat", s, 0), ("bt", 0), ("bt", 1)])
                    P.op("act", lambda e: e.activation(out=stat[s][:, 1:2], in_=stat[s][:, 0:1], func=AF.Ln,
                                                       scale=1.0 / D, bias=cst[:, 1:2]),
                         r=[("stat", s, 0), "cst"], w=[("stat", s, 1)])
                    P.op("act", lambda e: e.activation(out=stat[s][:, 2:3], in_=stat[s][:, 1:2], func=AF.Exp,
                                                       scale=-0.5, bias=cst[:, 2:3]),
                         r=[("stat", s, 1), "cst"], w=[("stat", s, 2)])

                def S1(tt=tt, st_=st_):
                    s = st_["s"]
                    P.op("dve", lambda e: e.tensor_scalar(out=hbf[s][:], in0=xt[s][:], scalar1=stat[s][:, 2:3],
                                                          scalar2=None, op0=ALU.mult),
                         r=[("xt", s), ("stat", s, 2)], w=["hbf"])
                    b = bank()
                    st_["b"] = b
                    pv = ps[b][:].bitcast(BF16)

                    def f_tr(e):
                        for dc in range(8):
                            i = e.transpose(pv[:, dc * 128:(dc + 1) * 128], hbf[s][:, dc * 128:(dc + 1) * 128], ident[:])
                        return i
                    P.op("pe", f_tr, r=["hbf", "ident"], x=[("ps", b)])

                def S2(tt=tt, st_=st_):
                    b = st_["b"]
                    pv = ps[b][:].bitcast(BF16)
                    P.op("dve", lambda e: e.tensor_tensor(out=xT[:, :, tt * 128:(tt + 1) * 128],
                                                          in0=pv.rearrange("p (c t) -> p c t", c=8),
                                                          in1=normw[:, :].unsqueeze(2).broadcast_to([128, 8, 128]), op=ALU.mult),
                         r=["normw"], w=[("xT", tt)], x=[("ps", b)])
                units.append([S0, S1, S2])
            pipeline(units, 3)

            ck(1)
            def proj_fm(s, c, tq, lo=0, hi=8):
                b = bank(lo, hi)

                def f(e, s=s, c=c, tq=tq, b=b):
                    for dc in range(8):
                        i = e.matmul(ps[b][:], lhsT=wsl[s][:, dc, c * 128:(c + 1) * 128],
                                     rhs=xT[:, dc, tq * 512:(tq + 1) * 512], start=(dc == 0), stop=(dc == 7))
                    return i
                P.op("pe", f, r=[("ws", s)] + [("xT", tq * 4 + i) for i in range(4)], x=[("ps", b)])
                return b

            def qk_unit(s, c, tq, wcol, fin):
                st_ = {}

                def S0():
                    b = proj_fm(s, c, tq, 0, 4)
                    st_["b"] = b
                    sq_ = bt()
                    st_["sq"] = sq_
                    P.op("act", lambda e: e.activation(out=BT[sq_][:], in_=ps[b][:], func=AF.Square), w=[("bt", sq_)], x=[("ps", b)])

                def S1():
                    sq_ = st_["sq"]
                    b2 = bank(4, 6)
                    P.op("pe", lambda e: e.matmul(ps[b2][:], lhsT=bones[:], rhs=BT[sq_][:], start=True, stop=True),
                         r=[("bt", sq_), "bones"], x=[("ps", b2)])
                    ln_ = ft()
                    P.op("act", lambda e: e.activation(out=FT[ln_][:], in_=ps[b2][:], func=AF.Ln, bias=cst[:, 1:2]),
                         r=["cst"], w=[("ft", ln_)], x=[("ps", b2)])
                    rs_ = ft()
                    st_["rs"] = rs_
                    P.op("act", lambda e: e.activation(out=FT[rs_][:], in_=FT[ln_][:], func=AF.Exp, scale=-0.5, bias=cst[:, 2:3]),
                         r=[("ft", ln_), "cst"], w=[("ft", rs_)])

                def S2():
                    b, rs_ = st_["b"], st_["rs"]
                    h_ = ft()
                    P.op("dve", lambda e: e.scalar_tensor_tensor(out=FT[h_][:], in0=ps[b][:], scalar=qkw[:, wcol:wcol + 1],
                                                               in1=FT[rs_][:], op0=ALU.mult, op1=ALU.mult),
                         r=[("ft", rs_), "qkw"], w=[("ft", h_)], x=[("ps", b)])
                    fin(h_)
                return [S0, S1, S2]

            def proj_v(col0):
                s = wload(col0)
                for tt in range(16):
                    b = bank()

                    def f(e, s=s, tt=tt, b=b):
                        for dc in range(8):
                            i = e.matmul(ps[b][:], lhsT=xT[:, dc, tt * 128:(tt + 1) * 128], rhs=wsl[s][:, dc, :],
                                         start=(dc == 0), stop=(dc == 7))
                        return i
                    P.op("pe", f, r=[("ws", s), ("xT", tt)], x=[("ps", b)])
                    P.op("act", lambda e, tt=tt, b=b: e.activation(out=VV[:, tt, :], in_=ps[b][:], func=AF.Copy),
                         w=[("V", tt)], x=[("ps", b)])

            def proj_plain(col0, dst, mk, scale):
                s = wload(col0)
                for tq in range(4):
                    for c in range(4):
                        b = proj_fm(s, c, tq)
                        P.op("dve", lambda e, c=c, tq=tq, b=b: e.tensor_scalar(out=dst[:, c, tq * 512:(tq + 1) * 512], in0=ps[b][:],
                                                                              scalar1=scale, scalar2=None, op0=ALU.mult),
                             w=mk(c, tq), x=[("ps", b)])

            def dump(nm, src_ap, keys):
                if debug:
                    P.dma("sp", "dbg", lambda e: e.dma_start(out=dbg[nm], in_=src_ap), r=keys)

            qakeys = [("qa", c, q) for c in range(4) for q in range(8)]
            qbkeys = [("qb", c, q) for c in range(4) for q in range(4)]

            s = wload(512)
            units = []
            for tq in range(4):
                for c in range(4):
                    def fin(h_, c=c, tq=tq):
                        P.op("dve", lambda e: e.tensor_copy(out=kk[:, c, tq * 512:(tq + 1) * 512], in_=FT[h_][:]),
                             r=[("ft", h_)], w=[("k", c, tq)])
                        P.op("dve", lambda e: e.tensor_reduce(
                            out=kmean[:, c, tq * 2:(tq + 1) * 2], in_=FT[h_][:].rearrange("p (a b) -> p a b", a=2), axis=AX.X, op=ALU.add),
                            r=[("ft", h_)], w=[("kmean", c, tq)])
                    units.append(qk_unit(s, c, tq, 1, fin))
            pipeline(units, 3)
            ck(2)
            s = wload(0)
            units = []
            for tq in range(4):
                gviews = [ps[6 + hh][:, 0:128].rearrange("p (i c n) -> p i c n", i=4, c=4) for hh in range(2)]
                for c in range(4):
                    def fin(h_, c=c, tq=tq, gviews=gviews):
                        P.op("dve", lambda e: e.tensor_scalar(out=qa[:, c, tq * 512:(tq + 1) * 512], in0=FT[h_][:], scalar1=0.125, scalar2=None, op0=ALU.mult),
                             r=[("ft", h_)], w=[("qa", c, 2 * tq), ("qa", c, 2 * tq + 1)])

                        def f_gate(e):
                            for i in range(4):
                                for hh in range(2):
                                    ins = e.matmul(gviews[hh][:, i, c, :], lhsT=FT[h_][hh * 64:(hh + 1) * 64, i * 128:(i + 1) * 128],
                                                   rhs=kmean[hh * 64:(hh + 1) * 64, c, :], start=True, stop=True)
                            return ins
                        P.op("pe", f_gate, r=[("ft", h_)] + [("kmean", c, t) for t in range(4)], x=[("ps", 6), ("ps", 7)])
                    units.append(qk_unit(s, c, tq, 0, fin))

                def topk(tq=tq, gviews=gviews):
                    for i in range(4):
                        tt = tq * 4 + i
                        qblk = tt // 2
                        sl = rr("g", 2)

                        def f_gm(e, sl=sl, i=i, qblk=qblk, gviews=gviews):
                            for hh in range(2):
                                ins = e.tensor_tensor(out=gm[sl][:, hh * 4:(hh + 1) * 4, :], in0=gviews[hh][:, i, :, :],
                                                      in1=cmneg[:, qblk:qblk + 1, :].broadcast_to([128, 4, 8]), op=ALU.add)
                            return ins
                        P.op("dve", f_gm, r=["cmneg"], w=[("gm", sl)], x=[("ps", 6), ("ps", 7)])

                        def f_m8(e, sl=sl):
                            for h in range(8):
                                ins = e.max(out=m8[sl][:, h, :], in_=gm[sl][:, h, :])
                            return ins
                        P.op("dve", f_m8, r=[("gm", sl)], w=[("m8", sl)])
                        P.op("dve", lambda e, sl=sl: e.tensor_tensor(out=sel[sl][:], in0=gm[sl][:], in1=m8[sl][:, :, 2:3].broadcast_to([128, 8, 8]), op=ALU.is_ge),
                             r=[("gm", sl), ("m8", sl)], w=[("sel", sl)])
                        P.op("dve", lambda e, sl=sl: e.tensor_scalar(out=sel[sl][:], in0=sel[sl][:], scalar1=-1.0, scalar2=-NEG, op0=ALU.add, op1=ALU.mult),
                             r=[("sel", sl)], w=[("sel", sl)])
                        P.op("dve", lambda e, sl=sl, qblk=qblk: e.tensor_tensor(out=sel[sl][:], in0=sel[sl][:], in1=cm01[:, qblk:qblk + 1, :].broadcast_to([128, 8, 8]), op=ALU.mult),
                             r=[("sel", sl), "cm01"], w=[("sel", sl)])
                        P.op("dve", lambda e, sl=sl: e.tensor_tensor(out=mv[sl][:], in0=sel[sl][:], in1=b31g[:, :].unsqueeze(2).broadcast_to([128, 8, 8]), op=ALU.add),
                             r=[("sel", sl), "b31g"], w=[("mv", sl)])
                        b = bank(4, 6)
                        pv = ps[b][:].bitcast(BF16)

                        def f_mt(e, sl=sl, pv=pv):
                            for hh in range(2):
                                for c in range(4):
                                    ins = e.transpose(pv[hh * 64:hh * 64 + 8, c * 128:(c + 1) * 128], mv[sl][:, hh * 4 + c, :], ident[:])
                            return ins
                        P.op("pe", f_mt, r=[("mv", sl), "ident"], x=[("ps", b)])

                        def f_mc(e, tt=tt, pv=pv):
                            for hh in range(2):
                                ins = e.activation(out=maskT[hh * 64:hh * 64 + 8, :, tt * 128:(tt + 1) * 128],
                                                   in_=pv[hh * 64:hh * 64 + 8, 0:512].rearrange("p (c t) -> p c t", c=4), func=AF.Copy)
                            return ins
                        P.op("act", f_mc, w=[("maskT", tt)], x=[("ps", b)])

                units.append([None, None, topk])
            pipeline(units, 3)
            ck(3)
            proj_v(1024)
            dump("qa", qa[:], qakeys)
            dump("ka", kk, kkeys)
            dump("va", VV, vkeys)
            dump("mask", maskT[:], [("maskT", t) for t in range(16)])

            ck(4)
            units = []
            it = 0
            for qblk in range(8):
                for c in range(4):
                    ob_, db_ = (4, 5) if it % 2 == 0 else (6, 7)
                    it += 1
                    for n in range(qblk + 1):
                        own = (n == qblk)
                        st_ = {}

                        def A(c=c, n=n, own=own, qblk=qblk, st_=st_):
                            zb = 2 * rr("mzb", 2)
                            st_["zb"] = zb
                            zp = PSP(zb)

                            def f_qk(e):
                                for hh in range(2):
                                    p0 = hh * 64
                                    zv = zp[:, hh, :].rearrange("p (k q) -> p k q", k=2)
                                    for kt in range(2):
                                        c0 = 128 if (own and kt == 1) else 0
                                        e.matmul(zv[:, kt, c0:256], lhsT=kk[p0:p0 + 64, c, (2 * n + kt) * 128:(2 * n + kt + 1) * 128],
                                                 rhs=qa[p0:p0 + 64, c, qblk * 256 + c0:(qblk + 1) * 256], start=True, stop=False)
                                        ins = e.matmul(zv[:, kt, c0:256], lhsT=esel[p0:p0 + 64, n, :], rhs=maskT[p0:p0 + 64, c, qblk * 256 + c0:(qblk + 1) * 256],
                                                       start=False, stop=True)
                                return ins
                            P.op("pe", f_qk, r=[("k", c, n // 2), ("qa", c, qblk), "esel", ("maskT", 2 * qblk), ("maskT", 2 * qblk + 1)],
                                 x=[("ps", zb), ("ps", zb + 1)])
                            if own or n == qblk - 1:
                                def f_b(e):
                                    for hh in range(2):
                                        h = 2 * c + hh
                                        zv = zp[:, hh, :].rearrange("p (k q) -> p k q", k=2)
                                        if own:
                                            e.tensor_tensor(out=zv[:, 0, 0:128], in0=zv[:, 0, 0:128], in1=biasD[:, h, :], op=ALU.add)
                                            e.tensor_tensor(out=zv[:, 0, 128:256], in0=zv[:, 0, 128:256], in1=biasO[:, h, :], op=ALU.add)
                                            ins = e.tensor_tensor(out=zv[:, 1, 128:256], in0=zv[:, 1, 128:256], in1=biasD[:, h, :], op=ALU.add)
                                        else:
                                            ins = e.tensor_tensor(out=zv[:, 1, 0:128], in0=zv[:, 1, 0:128], in1=biasO[:, h, :], op=ALU.add)
                                    return ins
                                P.op("dve", f_b, r=["biasD", "biasO"], x=[("ps", zb), ("ps", zb + 1)])
                            pp = btp()
                            st_["pp"] = pp
                            P.op("act", lambda e: e.activation(out=BTP(pp), in_=zp, func=AF.Exp, bias=cst[:, 2:3]),
                                 r=["cst"], w=[("bt", pp), ("bt", pp + 1)], x=[("ps", zb), ("ps", zb + 1)])

                        def B(c=c, n=n, own=own, qblk=qblk, st_=st_, ob_=ob_, db_=db_):
                            pp = st_["pp"]

                            def f_av(e):
                                for hh in range(2):
                                    p0 = hh * 64
                                    h = 2 * c + hh
                                    pvv = BTP(pp)[:, hh, :].rearrange("p (k q) -> p k q", k=2)
                                    for kt in range(2):
                                        c0 = 128 if (own and kt == 1) else 0
                                        first = (n == 0 and kt == 0)
                                        last = (own and kt == 1)
                                        e.matmul(ps[ob_][p0:p0 + 64, c0:256], lhsT=VV[:, 2 * n + kt, h * 64:(h + 1) * 64], rhs=pvv[:, kt, c0:256],
                                                 start=first, stop=last, skip_group_check=True)
                                        ins = e.matmul(ps[db_][p0:p0 + 64, c0:256], lhsT=ones[:, 0:64], rhs=pvv[:, kt, c0:256],
                                                       start=first, stop=last, skip_group_check=True)
                                return ins
                            P.op("pe", f_av, r=[("bt", pp), ("bt", pp + 1), ("V", 2 * n), ("V", 2 * n + 1), "ones"], x=[("ps", ob_), ("ps", db_)])
                            if own:
                                rd_ = ft()
                                P.op("dve", lambda e: e.reciprocal(out=FT[rd_][:, 0:256], in_=ps[db_][:, 0:256]), w=[("ft", rd_)], x=[("ps", db_)])
                                P.op("dve", lambda e: e.tensor_tensor(out=qa[:, c, qblk * 256:(qblk + 1) * 256], in0=ps[ob_][:, 0:256],
                                                                      in1=FT[rd_][:, 0:256], op=ALU.mult),
                                     r=[("ft", rd_)], w=[("qa", c, qblk)], x=[("ps", ob_)])
                        units.append([A, B])
            units = [[u_[0], None, u_[1]] for u_ in units]
            pipeline(units, 3)
            wstate["mask_free"] = True
            dump("oa", qa[:], qakeys)

            ck(5)
            proj_plain(2048, qb_, lambda c, tq: [("qb", c, tq)], 0.125)
            proj_plain(2560, kk, lambda c, tq: [("k", c, tq)], 1.0)
            proj_v(3072)
            dump("qb", qb_[:], qbkeys)
            dump("kb", kk, kkeys)
            dump("vb", VV, vkeys)

            ck(6)
            units = []
            for qc in range(4):
                for c in range(4):
                    ob_ = 6
                    nkb = 4 * qc + 4
                    for u in range(nkb):
                        kbk = nkb - 1 - u
                        j = kbk - 4 * qc
                        c0 = j * 128 if j >= 0 else 0
                        st_ = {}

                        def A(c=c, qc=qc, u=u, kbk=kbk, j=j, c0=c0, st_=st_):
                            zb = 2 * rr("szb", 2)
                            zp = PSP(zb)

                            def f_z(e):
                                for hh in range(2):
                                    p0 = hh * 64
                                    ins = e.matmul(zp[:, hh, c0:512], lhsT=kk[p0:p0 + 64, c, kbk * 128:(kbk + 1) * 128],
                                                   rhs=qb_[p0:p0 + 64, c, qc * 512 + c0:(qc + 1) * 512], start=True, stop=(j < 0))
                                    if j >= 0:
                                        ins = e.matmul(zp[:, hh, c0:c0 + 128], lhsT=ident[:], rhs=negm[:], start=False, stop=True)
                                return ins
                            P.op("pe", f_z, r=[("k", c, kbk // 4), ("qb", c, qc), "ident", "negm"], x=[("ps", zb), ("ps", zb + 1)])
                            st_["zb"] = zb

                        def A1(c=c, qc=qc, u=u, kbk=kbk, j=j, c0=c0, st_=st_):
                            zb = st_["zb"]
                            zp = PSP(zb)
                            e_ = 2 * rr('sbE', 3)
                            st_["e"] = e_
                            ekeys = [("ft", e_), ("ft", e_ + 1)]
                            P.op("act", lambda e: e.activation(out=FTP(e_)[:, :, c0:512], in_=zp[:, :, c0:512], func=AF.Exp, bias=cst[:, 2:3]),
                                 r=["cst"], w=ekeys, x=[("ps", zb), ("ps", zb + 1)])
                            L_ = btp()
                            st_["L"] = L_
                            P.op("act", lambda e: e.activation(out=BTP(L_)[:, :, c0:512], in_=FTP(e_)[:, :, c0:512], func=AF.Ln, bias=cst[:, 0:1]),
                                 r=ekeys + ["cst"], w=[("bt", L_), ("bt", L_ + 1)])

                        def B(c=c, qc=qc, u=u, kbk=kbk, c0=c0, st_=st_, nkb=nkb):
                            e_, L_ = st_["e"], st_["L"]
                            cp = PSP(4)
                            cprev, ccur = (u + 1) % 2, u % 2
                            if u == 0:
                                for i_ in range(2):
                                    P.op("dve", lambda e, i_=i_: e.memset(carry[i_][0:33, :], 0.0), w=[("carry", i_)])

                            def f_cs(e):
                                for hh in range(2):
                                    if u < nkb - 1:
                                        e.matmul(ps[7][:, c0:512], lhsT=csel[:, hh, :], rhs=BTP(L_)[:, hh, c0:512],
                                                 start=(u == 0 and hh == 0), stop=False, skip_group_check=True)
                                    ins = e.matmul(cp[:, hh, c0:512], lhsT=tri[:], rhs=BTP(L_)[:, hh, c0:512], start=True, stop=(u == 0))
                                    if u > 0:
                                        ins = e.matmul(cp[:, hh, c0:512], lhsT=rsel[:, hh, :], rhs=carry[cprev][:, c0:512], start=False, stop=True)
                                return ins
                            P.op("pe", f_cs, r=[("bt", L_), ("bt", L_ + 1), ("carry", cprev), "tri", "csel", "rsel"], x=[("ps", 4), ("ps", 5), ("ps", 7)])
                            if u < nkb - 1:
                                P.op("dve", lambda e: e.tensor_copy(out=carry[ccur][0:33, c0:512], in_=ps[7][0:33, c0:512]),
                                     w=[("carry", ccur)], x=[("ps", 7)])
                            g_ = 6 + 2 * rr('sbG', 2)
                            gkeys = [("ft", g_), ("ft", g_ + 1)]
                            P.op("act", lambda e: e.activation(out=FTP(g_)[:, :, c0:512], in_=cp[:, :, c0:512], func=AF.Exp, scale=-1.0, bias=cst[:, 2:3]),
                                 r=["cst"], w=gkeys, x=[("ps", 4), ("ps", 5)])
                            w_ = btp()
                            st_["w"] = w_
                            P.op("dve", lambda e: e.tensor_tensor(out=BTP(w_)[:, :, c0:512], in0=FTP(e_)[:, :, c0:512], in1=FTP(g_)[:, :, c0:512], op=ALU.mult),
                                 r=[("ft", e_), ("ft", e_ + 1)] + gkeys, w=[("bt", w_), ("bt", w_ + 1)])

                        def C(c=c, qc=qc, u=u, kbk=kbk, c0=c0, st_=st_, nkb=nkb, ob_=ob_):
                            w_ = st_["w"]

                            def f_av(e):
                                for hh in range(2):
                                    p0 = hh * 64
                                    h = 2 * c + hh
                                    ins = e.matmul(ps[ob_][p0:p0 + 64, c0:512], lhsT=VV[:, kbk, h * 64:(h + 1) * 64], rhs=BTP(w_)[:, hh, c0:512],
                                                   start=(u == 0), stop=(u == nkb - 1), skip_group_check=True)
                                return ins
                            P.op("pe", f_av, r=[("bt", w_), ("bt", w_ + 1), ("V", kbk)], x=[("ps", ob_)])
                            if u == nkb - 1:
                                P.op("dve", lambda e: e.tensor_copy(out=qb_[:, c, qc * 512:(qc + 1) * 512], in_=ps[ob_][:]),
                                     w=[("qb", c, qc)], x=[("ps", ob_)])
                        units.append([A, A1, B, C])
            pipeline(units, 4)
            dump("ob", qb_[:], qbkeys)

            ck(7)
            P.dma("pool", "w6", lambda e: e.dma_start(out=wupa, in_=wua_d.rearrange("(c p) n -> p c n", p=128)), w=["wup"])
            P.dma("pool", "w6", lambda e: e.dma_start(out=wupb, in_=wub_d.rearrange("(c p) n -> p c n", p=128)), w=["wup"])
            P.dma("pool", "w6", lambda e: e.dma_start(out=wout, in_=wo_d.rearrange("(c p) n -> p c n", p=128)), w=["wout"])
            w6keys = ["wup", "wout"]
            units = []
            for tq in range(4):
                xkeys = [("xT", tq * 4 + i) for i in range(4)]
                for br, (zcol, Osrc) in enumerate(((1536, qa), (3584, qb_))):
                    slot_ = {}
                    for c in range(4):
                        st_ = {}

                        def S0(br=br, zcol=zcol, c=c, tq=tq, st_=st_, slot_=slot_):
                            if c == 0:
                                slot_["s"] = wload(zcol)
                            st_["b"] = proj_fm(slot_["s"], c, tq)

                        def S1(br=br, Osrc=Osrc, c=c, tq=tq, st_=st_):
                            b = st_["b"]
                            sg_ = 4 + rr('ft6', 4)
                            P.op("act", lambda e: e.activation(out=FT[sg_][:], in_=ps[b][:], func=AF.Sigmoid, bias=cst[:, 2:3]),
                                 r=["cst"], w=[("ft", sg_)], x=[("ps", b)])
                            t_ = 4 + rr('ft6', 4)
                            P.op("dve", lambda e: e.tensor_tensor(out=FT[t_][:], in0=ps[b][:], in1=FT[sg_][:], op=ALU.mult),
                                 r=[("ft", sg_)], w=[("ft", t_)], x=[("ps", b)])
                            okeys = [("qa", c, 2 * tq), ("qa", c, 2 * tq + 1)] if br == 0 else [("qb", c, tq)]
                            P.op("dve", lambda e: e.tensor_tensor(out=Gt[br][:, c, :], in0=FT[t_][:],
                                                                  in1=Osrc[:, c, tq * 512:(tq + 1) * 512], op=ALU.mult),
                                 r=[("ft", t_)] + okeys, w=[("bt", br * 4 + c)])
                        units.append([S0, S1])
                units.append([None, None])
                for half in range(2):
                    slot_ = {}
                    for f4 in range(4):
                        f = half * 4 + f4
                        st_ = {}

                        def S0(half=half, f4=f4, f=f, tq=tq, st_=st_, slot_=slot_, xkeys=xkeys):
                            if f4 == 0:
                                slot_["a"] = wload(4096 + half * 512)
                                slot_["b"] = wload(5120 + half * 512)
                            bya, byb, bga, bgb = bank(), bank(), bank(), bank()
                            st_["banks"] = (bya, byb, bga, bgb)
                            for br, (bg, sgw) in enumerate(((bga, slot_["a"]), (bgb, slot_["b"]))):
                                def f_g(e, bg=bg, sgw=sgw):
                                    for dc in range(8):
                                        ins = e.matmul(ps[bg][:], lhsT=wsl[sgw][:, dc, f4 * 128:(f4 + 1) * 128],
                                                       rhs=xT[:, dc, tq * 512:(tq + 1) * 512], start=(dc == 0), stop=(dc == 7))
                                    return ins
                                P.op("pe", f_g, r=[("ws", sgw)] + xkeys, x=[("ps", bg)])
                            for br, (by, wup_) in enumerate(((bya, wupa), (byb, wupb))):
                                def f_up(e, by=by, wup_=wup_, br=br):
                                    for kc in range(4):
                                        ins = e.matmul(ps[by][:], lhsT=wup_[:, kc, f * 128:(f + 1) * 128], rhs=Gt[br][:, kc, :], start=(kc == 0), stop=(kc == 3))
                                    return ins
                                P.op("pe", f_up, r=w6keys + [("bt", br * 4 + kc) for kc in range(4)], x=[("ps", by)])

                        def S1(f=f, st_=st_):
                            bya, byb, bga, bgb = st_["banks"]
                            tms = []
                            for br, (bg, by) in enumerate(((bga, bya), (bgb, byb))):
                                sg_ = 4 + rr('ft6', 4)
                                P.op("act", lambda e, sg_=sg_, bg=bg, br=br: e.activation(out=FT[sg_][:], in_=ps[bg][:], func=AF.Sigmoid, bias=mgb[:, br * 8 + f: br * 8 + f + 1]),
                                     r=["mgb"], w=[("ft", sg_)], x=[("ps", bg)])
                                tm_ = 4 + rr('ft6', 4)
                                tms.append(tm_)
                                P.op("dve", lambda e, sg_=sg_, tm_=tm_, by=by: e.tensor_tensor(out=FT[tm_][:], in0=ps[by][:], in1=FT[sg_][:], op=ALU.mult),
                                     r=[("ft", sg_)], w=[("ft", tm_)], x=[("ps", by)])
                            P.op("dve", lambda e, ta=tms[0], tb=tms[1]: e.tensor_tensor(out=yT[:, f, :], in0=FT[ta][:], in1=FT[tb][:], op=ALU.add),
                                 r=[("ft", tms[0]), ("ft", tms[1])], w=[("ft", f // 2)])
                        units.append([S0, S1])
                units.append([None, None])
                for i in range(4):
                    tt = tq * 4 + i
                    xs_ = {}
                    for half in range(2):
                        st_ = {}

                        def S0(i=i, tt=tt, half=half, st_=st_, xs_=xs_):
                            if half == 0:
                                s = rr("x", 2)
                                xs_["s"] = s
                                src = x_d[t0 + tt * 128: t0 + (tt + 1) * 128, :]
                                P.dma("sp", f"x{s}", lambda e: e.dma_start(out=xt[s][:], in_=src), w=[("xt", s)])
                            b = bank()
                            st_["b"] = b

                            def f_o(e):
                                for fc in range(8):
                                    ins = e.matmul(ps[b][:], lhsT=yT[:, fc, i * 128:(i + 1) * 128], rhs=wout[:, fc, half * 512:(half + 1) * 512],
                                                   start=(fc == 0), stop=(fc == 7))
                                return ins
                            P.op("pe", f_o, r=w6keys + [("ft", fc) for fc in range(4)], x=[("ps", b)])

                        def S1(tt=tt, half=half, st_=st_, xs_=xs_):
                            b, s = st_["b"], xs_["s"]
                            P.op("dve", lambda e: e.tensor_tensor(out=xt[s][:, half * 512:(half + 1) * 512], in0=ps[b][:],
                                                                  in1=xt[s][:, half * 512:(half + 1) * 512], op=ALU.add),
                                 r=[("xt", s)], w=[("xt", s)], x=[("ps", b)])
                            if half == 1:
                                dst = out_d[t0 + tt * 128: t0 + (tt + 1) * 128, :]
                                P.dma("sp", f"st{s}", lambda e: e.dma_start(out=dst, in_=xt[s][:]), r=[("xt", s)])
                        units.append([S0, S1])
            pipeline(units, 2)

        try:
            for seq in range(nseq):
                seq_body(seq)
        except _Stop:
            pass
        print('ops', {k: len(v) for k, v in P.ops.items()})
        block = es.enter_context(nc.Block())
        for en in COMPUTE:
            k = 0
            for op in P.ops[en]:
                if op.signal and op.dma is None:
                    k += 1
                    op.sig_idx = k

        def emit(en, eh):
            for op in P.ops[en]:
                for key, val in op.waits.items():
                    if isinstance(key, tuple):
                        eh.wait_ge(sem("d_" + key[1]), val)
                    else:
                        eh.wait_ge(sem("c_" + key), P.ops[key][val - 1].sig_idx)
                ins = op.fn(eh)
                if op.dma is not None:
                    ins.then_inc(sem("d_" + op.dma[0]), 16)
                elif op.signal:
                    ins.then_inc(sem("c_" + en), 1)
            if en == "sp":
                for name, c in P.dma_count.items():
                    if name.startswith("st") or name == "dbg":
                        eh.wait_ge(sem("d_" + name), c)

        for name in P.dma_count:
            sem("d_" + name)

        @block.tensor
        def _(e):
            emit("pe", e)

        @block.scalar
        def _(e):
            emit("act", e)

        @block.vector
        def _(e):
            emit("dve", e)

        @block.gpsimd
        def _(e):
            emit("pool", e)

        @block.sync
        def _(e):
            emit("sp", e)
    return nc


_CONST = None


def _consts():
    global _CONST
    if _CONST is None:
        bf = ml_dtypes.bfloat16
        i = np.arange(128)
        c = {}
        c["ident"] = np.eye(128, dtype=np.float32).astype(bf)
        c["bones"] = ((i[:, None] // 64 == i[None, :] // 64).astype(np.float32) / 64.0).astype(bf)
        c["tri"] = (i[:, None] >= i[None, :]).astype(np.float32).astype(bf)
        c["ones"] = np.ones((128, 128), np.float32).astype(bf)
        c["smask"] = (i[:, None] < i[None, :]).astype(np.float32)
        es = np.zeros((128, 8, 128), np.float32)
        for n in range(8):
            es[n, n, :] = 1.0
            es[64 + n, n, :] = 1.0
        c["esel"] = es.astype(bf)
        n = np.arange(8)
        cm = (n[None, :] < n[:, None]).astype(np.float32)
        c["cm01"] = np.ascontiguousarray(np.broadcast_to(cm[None], (128, 8, 8))).astype(np.float32)
        c["cmneg"] = np.ascontiguousarray(np.broadcast_to(((1.0 - cm) * -1e30)[None], (128, 8, 8))).astype(np.float32)
        cs_ = np.zeros((128, 2, 128), np.float32)
        rs_ = np.zeros((128, 2, 128), np.float32)
        for hh in range(2):
            cs_[:, hh, 32 * hh] = 1.0
            rs_[32 * hh, hh, :] = 1.0
        c["csel"] = cs_.astype(bf)
        c["rsel"] = rs_.astype(bf)
        c["negm"] = np.where(i[:, None] >= i[None, :], NEG, 0.0).astype(np.float32).astype(bf)
        cst = np.zeros((128, 4), np.float32)
        cst[:, 0] = 1.0
        cst[:, 1] = 1e-6
        c["cst"] = cst
        bk = _bucket_table(512)
        d = i[None, :] - i[:, None]
        c["idxD"] = bk[np.maximum(d, 0)]
        c["mskD"] = d >= 0
        c["idxO"] = bk[128 + d]
        _CONST = c
    return _CONST


def _prep(inputs):
    c = _consts()
    f32 = np.float32
    rb = np.asarray(inputs["rel_bias"], f32)
    m = {
        "w_in": np.ascontiguousarray(np.asarray(inputs["w_in"], f32)[0]),
        "w_up_a": np.ascontiguousarray(np.asarray(inputs["w_up_moba"], f32)[0]),
        "w_up_b": np.ascontiguousarray(np.asarray(inputs["w_up_sb"], f32)[0]),
        "w_out": np.ascontiguousarray(np.asarray(inputs["w_out"], f32)[0]),
        "normw": np.ascontiguousarray(np.asarray(inputs["norm_w"], f32)[0].reshape(8, 128).T),
        "qkw": np.ascontiguousarray(np.stack([np.tile(np.asarray(inputs["q_norm_w"], f32)[0], 2),
                                              np.tile(np.asarray(inputs["k_norm_w"], f32)[0], 2)], axis=1)),
        "mgb": np.ascontiguousarray(np.asarray(inputs["merge_gate_b"], f32)[0].reshape(16, 128).T),
        "b31": np.ascontiguousarray(np.broadcast_to(rb[None, :, 31], (128, 8))),
        "b31g": np.ascontiguousarray(np.broadcast_to(rb[None, [0, 2, 4, 6, 1, 3, 5, 7], 31], (128, 8))),
    }
    bd = rb[:, c["idxD"]]
    bd = np.where(c["mskD"][None], bd, f32(NEG))
    m["biasD"] = np.ascontiguousarray(bd.transpose(1, 0, 2)).astype(f32)
    m["biasO"] = np.ascontiguousarray(rb[:, c["idxO"]].transpose(1, 0, 2)).astype(f32)
    for k in ("cst", "ident", "bones", "tri", "ones", "smask", "esel", "cmneg", "cm01", "csel", "rsel", "negm"):
        m[k] = c[k]
    return m


_NC = {}


def kernel(**inputs):
    x = np.asarray(inputs["x"], np.float32)
    B = x.shape[0]
    ncores = 8
    nseq = B // ncores
    shared = _prep(inputs)
    if nseq not in _NC:
        _NC[nseq] = build(nseq)
    nc = _NC[nseq]
    in_maps = []
    for i in range(ncores):
        m = dict(shared)
        m["x"] = np.ascontiguousarray(x[i * nseq:(i + 1) * nseq].reshape(nseq * S, D))
        in_maps.append(m)
    res = run_bass_kernel_spmd(nc, in_maps, core_ids=list(range(ncores)))
    out = np.stack([np.asarray(r["out"], np.float32).reshape(nseq, S, D) for r in res.results], axis=0)
    return out.reshape(B, S, D)
```
